# Optimizing a Trainium2 kernel written in Bass

```python
import math
import jax, jax.numpy as jnp
from jax import lax
import numpy as np

D_MODEL = 1024
BATCH = 4
SEQ = 4096
DEPTH = 2

GRID_W = 64
CTX_LEN = 256
HEAD_DIM = 64
N_Q_HEADS = 8
N_KV_HEADS = 2
Q_PER_KV = N_Q_HEADS // N_KV_HEADS
ATTN_WIDTH = N_Q_HEADS * HEAD_DIM
KV_WIDTH = N_KV_HEADS * HEAD_DIM
N_FOURIER_GROUPS = 8
FOURIER_GROUP_DIM = 64
FOURIER_WIDTH = N_FOURIER_GROUPS * FOURIER_GROUP_DIM
MIX_WIDTH = ATTN_WIDTH + FOURIER_WIDTH
IN_WIDTH = ATTN_WIDTH + 2 * KV_WIDTH + FOURIER_WIDTH
WINDOW = 128
BLOCK = 128
FFN_HIDDEN = -(-8 * D_MODEL // (3 * 256)) * 256
ROPE_THETA = 10000.0
ROPE_PAIRS_PER_AXIS = HEAD_DIM // 4
RMS_EPS = 1e-6
N_MOD = 6
NEG_INF = -1e30

kernel_name = 'hybrid_fourier_windowed_gqa_dit_block'


def rmsnorm(x, g):
    xf = x.astype(jnp.float32)
    y = xf * lax.rsqrt(jnp.mean(xf * xf, axis=-1, keepdims=True) + RMS_EPS)
    return (y * g.astype(jnp.float32)).astype(x.dtype)


def modulate(h, shift, scale):
    return h * (1.0 + scale) + shift


def axial_rope_tables(rows):
    row = jnp.broadcast_to(jnp.arange(rows)[:, None], (rows, GRID_W)).reshape(-1).astype(jnp.float32)
    col = jnp.broadcast_to(jnp.arange(GRID_W)[None, :], (rows, GRID_W)).reshape(-1).astype(jnp.float32)
    freqs = ROPE_THETA ** (-jnp.arange(ROPE_PAIRS_PER_AXIS, dtype=jnp.float32) / ROPE_PAIRS_PER_AXIS)
    ang = jnp.concatenate([row[:, None] * freqs, col[:, None] * freqs], axis=-1)
    return jnp.cos(ang), jnp.sin(ang)


def apply_axial_rope(x, cos, sin):
    xf = x.astype(jnp.float32).reshape(*x.shape[:-1], HEAD_DIM // 2, 2)
    x1, x2 = xf[..., 0], xf[..., 1]
    c = cos[None, :, None, :]
    s = sin[None, :, None, :]
    out = jnp.stack([x1 * c - x2 * s, x1 * s + x2 * c], axis=-1).reshape(x.shape)
    return out.astype(x.dtype)


def sink_column(sink, lead_shape):
    sk = sink.astype(jnp.float32).reshape(N_KV_HEADS, Q_PER_KV)[:, :, None, None]
    return jnp.broadcast_to(sk, lead_shape + (1,))


def windowed_gqa_with_context(q, k, v, kc, vc, sink):
    B, S = q.shape[0], q.shape[1]
    nb = S // BLOCK
    scale = HEAD_DIM ** -0.5
    qb = q.reshape(B, nb, BLOCK, N_KV_HEADS, Q_PER_KV, HEAD_DIM)
    pad = ((0, 0), (BLOCK, BLOCK), (0, 0), (0, 0))
    kp = jnp.pad(k, pad).reshape(B, nb + 2, BLOCK, N_KV_HEADS, HEAD_DIM)
    vp = jnp.pad(v, pad).reshape(B, nb + 2, BLOCK, N_KV_HEADS, HEAD_DIM)
    kb = jnp.concatenate([kp[:, :-2], kp[:, 1:-1], kp[:, 2:]], axis=2)
    vb = jnp.concatenate([vp[:, :-2], vp[:, 1:-1], vp[:, 2:]], axis=2)
    s_loc = jnp.einsum('bnqkgd,bnjkd->bnkgqj', qb, kb).astype(jnp.float32) * scale
    q_pos = jnp.arange(nb)[:, None] * BLOCK + jnp.arange(BLOCK)[None, :]
    k_pos = (jnp.arange(nb)[:, None] - 1) * BLOCK + jnp.arange(3 * BLOCK)[None, :]
    valid = (jnp.abs(q_pos[:, :, None] - k_pos[:, None, :]) <= WINDOW) & (k_pos[:, None, :] >= 0) & (k_pos[:, None, :] < S)
    s_loc = jnp.where(valid[None, :, None, None], s_loc, NEG_INF)
    s_ctx = jnp.einsum('bnqkgd,bckd->bnkgqc', qb, kc).astype(jnp.float32) * scale
    logits = jnp.concatenate([s_loc, s_ctx, sink_column(sink, s_loc.shape[:-1])], axis=-1)
    p = jax.nn.softmax(logits, axis=-1)
    n_loc = 3 * BLOCK
    n_ctx = kc.shape[1]
    p_loc = p[..., :n_loc].astype(v.dtype)
    p_ctx = p[..., n_loc:n_loc + n_ctx].astype(v.dtype)
    out = jnp.einsum('bnkgqj,bnjkd->bnqkgd', p_loc, vb) + jnp.einsum('bnkgqc,bckd->bnqkgd', p_ctx, vc)
    return out.reshape(B, S, ATTN_WIDTH)


def context_gqa(qc, kc, vc, sink):
    B, C = qc.shape[0], qc.shape[1]
    scale = HEAD_DIM ** -0.5
    qg = qc.reshape(B, C, N_KV_HEADS, Q_PER_KV, HEAD_DIM)
    s = jnp.einsum('bqkgd,bckd->bkgqc', qg, kc).astype(jnp.float32) * scale
    p = jax.nn.softmax(jnp.concatenate([s, sink_column(sink, s.shape[:-1])], axis=-1), axis=-1)[..., :C]
    out = jnp.einsum('bkgqc,bckd->bqkgd', p.astype(vc.dtype), vc)
    return out.reshape(B, C, ATTN_WIDTH)


def fourier_mix(u, w_four):
    B, N = u.shape[0], u.shape[1]
    ug = u.reshape(B, N, N_FOURIER_GROUPS, FOURIER_GROUP_DIM).astype(jnp.float32)
    mixed = jnp.fft.fft2(ug, axes=(1, 3), norm='ortho').real.astype(u.dtype)
    return jnp.einsum('bngc,gcd->bngd', mixed, w_four).reshape(B, N, FOURIER_WIDTH)


def swiglu(h, w_gate, w_up, w_down):
    return (jax.nn.silu(h @ w_gate) * (h @ w_up)) @ w_down


def setup_inputs(seed: int = 0) -> dict:
    key = jax.random.key(seed)
    ks = jax.random.split(key, 17)
    f32 = jnp.float32
    D = D_MODEL

    def nrm(k, shape, s):
        return jax.random.normal(k, shape, f32) * s

    return {
        'x': nrm(ks[0], (BATCH, SEQ, D), 1.0),
        'c': nrm(ks[1], (BATCH, D), 1.0),
        'ctx': nrm(ks[2], (BATCH, CTX_LEN, D), 1.0),
        'c_ctx': nrm(ks[3], (D,), 1.0),
        'w_ada': nrm(ks[4], (DEPTH, D, N_MOD * D), 0.5 * D ** -0.5),
        'b_ada': nrm(ks[5], (DEPTH, N_MOD * D), 0.02),
        'norm_pre_mix': 1.0 + nrm(ks[6], (DEPTH, D), 0.05),
        'norm_post_mix': 1.0 + nrm(ks[7], (DEPTH, D), 0.05),
        'norm_pre_ffn': 1.0 + nrm(ks[8], (DEPTH, D), 0.05),
        'norm_post_ffn': 1.0 + nrm(ks[9], (DEPTH, D), 0.05),
        'w_in': nrm(ks[10], (DEPTH, D, IN_WIDTH), D ** -0.5),
        'w_out': nrm(ks[11], (DEPTH, MIX_WIDTH, D), MIX_WIDTH ** -0.5),
        'w_fourier': nrm(ks[12], (DEPTH, N_FOURIER_GROUPS, FOURIER_GROUP_DIM, FOURIER_GROUP_DIM), FOURIER_GROUP_DIM ** -0.5),
        'sink': nrm(ks[13], (DEPTH, N_Q_HEADS), 0.5),
        'w_gate': nrm(ks[14], (DEPTH, D, FFN_HIDDEN), D ** -0.5),
        'w_up': nrm(ks[15], (DEPTH, D, FFN_HIDDEN), D ** -0.5),
        'w_down': nrm(ks[16], (DEPTH, FFN_HIDDEN, D), FFN_HIDDEN ** -0.5),
    }


def reference(x, c, ctx, c_ctx, w_ada, b_ada, norm_pre_mix, norm_post_mix, norm_pre_ffn, norm_post_ffn,
              w_in, w_out, w_fourier, sink, w_gate, w_up, w_down):
    B, S, D = x.shape
    ROWS = S // GRID_W
    cos, sin = axial_rope_tables(ROWS)
    xc = ctx
    kv_lo, kv_hi = ATTN_WIDTH, ATTN_WIDTH + 2 * KV_WIDTH
    for i in range(DEPTH):
        update_ctx = i < DEPTH - 1
        mod_lat = jax.nn.silu(c) @ w_ada[i] + b_ada[i]
        mod_ctx = jax.nn.silu(c_ctx) @ w_ada[i] + b_ada[i]
        sh1, sc1, g1, sh2, sc2, g2 = [m[:, None, :] for m in jnp.split(mod_lat, N_MOD, axis=-1)]
        csh1, csc1, cg1, csh2, csc2, cg2 = jnp.split(mod_ctx, N_MOD, axis=-1)

        h = modulate(rmsnorm(x, norm_pre_mix[i]), sh1, sc1)
        hc = modulate(rmsnorm(xc, norm_pre_mix[i]), csh1, csc1)
        p = h @ w_in[i]
        q = apply_axial_rope(p[..., :ATTN_WIDTH].reshape(B, S, N_Q_HEADS, HEAD_DIM), cos, sin)
        k = apply_axial_rope(p[..., kv_lo:kv_lo + KV_WIDTH].reshape(B, S, N_KV_HEADS, HEAD_DIM), cos, sin)
        v = p[..., kv_lo + KV_WIDTH:kv_hi].reshape(B, S, N_KV_HEADS, HEAD_DIM)
        u = p[..., kv_hi:]
        kvc = hc @ w_in[i][:, kv_lo:kv_hi]
        C = xc.shape[1]
        kc = kvc[..., :KV_WIDTH].reshape(B, C, N_KV_HEADS, HEAD_DIM)
        vc = kvc[..., KV_WIDTH:].reshape(B, C, N_KV_HEADS, HEAD_DIM)

        attn = windowed_gqa_with_context(q, k, v, kc, vc, sink[i])
        four = fourier_mix(u, w_fourier[i])
        mix = jnp.concatenate([attn, four], axis=-1) @ w_out[i]
        x_mid = x + g1 * rmsnorm(mix, norm_post_mix[i])

        hf = modulate(rmsnorm(x_mid, norm_pre_ffn[i]), sh2, sc2)
        x = x_mid + g2 * rmsnorm(swiglu(hf, w_gate[i], w_up[i], w_down[i]), norm_post_ffn[i])

        if update_ctx:
            qc = (hc @ w_in[i][:, :ATTN_WIDTH]).reshape(B, C, N_Q_HEADS, HEAD_DIM)
            uc = hc @ w_in[i][:, kv_hi:]
            attn_c = context_gqa(qc, kc, vc, sink[i])
            four_c = fourier_mix(uc, w_fourier[i])
            mix_c = jnp.concatenate([attn_c, four_c], axis=-1) @ w_out[i]
            xc_mid = xc + cg1 * rmsnorm(mix_c, norm_post_mix[i])
            hfc = modulate(rmsnorm(xc_mid, norm_pre_ffn[i]), csh2, csc2)
            xc = xc_mid + cg2 * rmsnorm(swiglu(hfc, w_gate[i], w_up[i], w_down[i]), norm_post_ffn[i])
    return x
```

```python
import contextlib
import numpy as np
import ml_dtypes
import concourse.bass as bass
import concourse.mybir as mybir
from concourse.bass_utils import run_bass_kernel_spmd

F32 = mybir.dt.float32
BF16 = mybir.dt.bfloat16
ALU = mybir.AluOpType
AF = mybir.ActivationFunctionType
NPBF = ml_dtypes.bfloat16

ENGS = ["tensor", "vector", "scalar", "gpsimd", "sync"]
CH = 4000
EPS = 1e-6
NLAT = 2048
NCTX = 256
NTOK = NLAT + NCTX
ARENA_BYTES = 117760
RING_BYTES = 11264


class Sched:
    def __init__(self):
        self.stream = {e: [] for e in ENGS}
        self.cnt = {e: 0 for e in ENGS}
        self.last_w = {}
        self.readers = {}
        self.waited = {e: {} for e in ENGS}
        self.dma_cnt = {}
        self.latest = {}

    def _deps(self, reads, writes):
        deps = []
        for r in reads:
            t = self.last_w.get(r)
            if t:
                deps.append(t)
        for w in writes:
            t = self.last_w.get(w)
            if t:
                deps.append(t)
            for k, n in self.readers.get(w, {}).items():
                deps.append((k[0], k[1], n))
        return deps

    def _waits(self, eng, deps):
        best = {}
        for kind, name, n in deps:
            if eng == "tensor" and kind == "E" and name == "tensor":
                continue
            key = (kind, name)
            best[key] = max(best.get(key, 0), n)
        for key, n in best.items():
            if self.waited[eng].get(key, 0) >= n:
                continue
            self.waited[eng][key] = n
            self.stream[eng].append(("wait", (key[0], key[1], n)))

    def _commit(self, tok, reads, writes):
        key = (tok[0], tok[1])
        for r in reads:
            d = self.readers.setdefault(r, {})
            d[key] = max(d.get(key, 0), tok[2])
        for w in writes:
            self.last_w[w] = tok
            self.readers[w] = {}
        self.latest[key] = tok

    def op(self, eng, fns, reads=(), writes=()):
        if callable(fns):
            fns = [fns]
        self._waits(eng, self._deps(reads, writes))
        for fn in fns:
            self.cnt[eng] += 1
            self.stream[eng].append(("ins", fn, ("E", eng, self.cnt[eng])))
        self._commit(("E", eng, self.cnt[eng]), reads, writes)

    def dma(self, eng, fn, tag, reads=(), writes=(), inc=16):
        self._waits(eng, self._deps(reads, writes))
        n = self.dma_cnt.get(tag, 0) + inc
        self.dma_cnt[tag] = n
        tok = ("D", tag, n)
        self.stream[eng].append(("dma", fn, tok, inc))
        self._commit(tok, reads, writes)

    def barrier(self, exclude=()):
        toks = [t for k, t in self.latest.items() if k not in exclude]
        for e in ENGS:
            self._waits(e, toks)

    @staticmethod
    def semkey(tok):
        kind, name, n = tok
        if kind == "E":
            return ("E", name, (n - 1) // CH), (n - 1) % CH + 1
        return ("D", name), n

    def all_semkeys(self):
        keys = []
        for e in ENGS:
            for k in range((self.cnt[e] + CH - 1) // CH):
                keys.append(("E", e, k))
        for tag in self.dma_cnt:
            keys.append(("D", tag))
        return keys

    def emit(self, eng, handle, sems):
        for item in self.stream[eng]:
            if item[0] == "wait":
                k, v = self.semkey(item[1])
                handle.wait_ge(sems[k], v)
            elif item[0] == "ins":
                k, v = self.semkey(item[2])
                item[1](handle).then_inc(sems[k], 1)
            else:
                k, v = self.semkey(item[2])
                if item[3] == 16:
                    item[1](handle).then_inc(sems[k], 16)
                else:
                    item[1](handle).then_inc(sems[k])


_STOP = None
_SKIP = set()


def build_program():
    nc = bass.Bass("TRN2", target_bir_lowering=False)
    S = Sched()
    op, dma = S.op, S.dma

    def din(name, shape, dt=F32):
        return nc.dram_tensor(name, shape, dt, kind="ExternalInput").ap()

    xT_d = din("xT", [8, 128, NLAT])
    xcT_d = din("xcT", [8, 128, NCTX])
    cvec_d = din("cvec", [128, 16])
    w_ada_d = din("w_adas", [1, 1024, 3072])
    w_ada1_d = din("w_ada1", [1, 1024, 6144])
    b_ada_d = din("b_adaT", [128, 96])
    gains_d = din("gains", [128, 64])
    w_in_d = din("w_in", [2, 1024, 1280])
    w_out_d = din("w_out", [2, 1024, 1024])
    w_f_d = din("w_four", [2, 8, 64, 64])
    sink_d = din("sinkb", [128, 2 * 2 * 512])
    w_gate_d = din("w_gate", [2, 1024, 2816])
    w_up_d = din("w_up", [2, 1024, 2816])
    w_down_d = din("w_down", [2, 2816, 1024])
    perm_d = din("permT", [128, 128], BF16)
    rope_d = din("rope", [128, 2 * NLAT], BF16)
    masks_d = din("masks", [128, 4 * 512], BF16)
    F1_d = din("F1", [128, 128], BF16)
    G_d = din("G", [128, 64 * 64], BF16)
    CS64_d = din("CS64", [64, 128], F32)
    CS256_d = din("CS256", [128, 2 * 2 * 256], BF16)
    outT_d = nc.dram_tensor("outT", [8, 128, NLAT], F32, kind="ExternalOutput").ap()
    pay = [nc.dram_tensor(f"pay{i}", [8192, 128], BF16) for i in range(2)]
    gath = [nc.dram_tensor(f"gath{i}", [16384, 128], BF16) for i in range(2)]
    wg_bf = [nc.dram_tensor(f"wgbf{i}", [6, 128, 8, 512], BF16) for i in range(2)]
    wu_bf = [nc.dram_tensor(f"wubf{i}", [6, 128, 8, 512], BF16) for i in range(2)]
    wd_bf = [nc.dram_tensor(f"wdbf{i}", [4, 128, 22, 256], BF16) for i in range(2)]
    payM = nc.dram_tensor("payM", [128, 48], F32)
    gathM = nc.dram_tensor("gathM", [256, 48], F32)
    payE = [nc.dram_tensor(f"payE{i}", [512, 128], BF16) for i in range(2)]
    gathE = [nc.dram_tensor(f"gathE{i}", [1024, 128], BF16) for i in range(2)]

    es = contextlib.ExitStack()
    with es:
        es.enter_context(nc.allow_low_precision("bf16 matmul operands, fp32 accumulation"))
        es.enter_context(nc.allow_non_contiguous_dma("strided layouts"))

        def sb(name, shape, dt):
            return es.enter_context(nc.sbuf_tensor(name, shape, dt))

        xT = sb("xT_sb", [128, 8, NTOK], F32)
        ones_mean = sb("ones_mean", [128, 128], BF16)
        permT = sb("permT_sb", [128, 128], BF16)
        F1 = sb("F1_sb", [128, 128], BF16)
        CS64 = sb("CS64_sb", [64, 2, 64], F32)
        cvec = sb("cvec_sb", [128, 8, 2], F32)
        scb = sb("scb", [128, 8, 2], BF16)
        mpart = sb("mpart", [128, 48], F32)
        modall = sb("modall", [128, 2, 48], F32)
        m1stage = sb("m1stage", [128, 96], F32)
        b_adaT = sb("b_adaT_sb", [128, 2, 48], F32)
        gains = sb("gains_sb", [128, 2, 4, 8], F32)
        modT = sb("modT", [128, 48, 2], F32)
        vec = sb("vec", [128, 2, 4, 8], F32)
        epsc = sb("epsc", [128, 1], F32)
        sqb = sb("sqb", [128, 8, 512], BF16)
        rstd = sb("rstd", [128, 512], F32)
        tmp = [sb(f"tmp{i}", [128, 512], F32) for i in range(2)]
        arena = sb("arena", [128, ARENA_BYTES // 2], BF16)
        ps = [es.enter_context(nc.psum_tensor(f"ps{i}", [128, 512], F32)) for i in range(8)]

        def A(off, shape, dt):
            esz = 2 if dt == BF16 else 4
            nbytes = int(np.prod(shape[1:])) * esz
            assert off % 4 == 0 and off + nbytes <= ARENA_BYTES, (off, shape)
            v = arena[0:shape[0], off // 2:(off + nbytes) // 2]
            if dt == F32:
                v = v.bitcast(F32)
            if len(shape) == 2:
                return v
            names = "abcd"[:len(shape) - 1]
            pat = "p (" + " ".join(names) + ") -> p " + " ".join(names)
            return v.rearrange(pat, **{n: s for n, s in zip(names, shape[1:])})

        qT = A(0, [128, 16, 4, 128], BF16)
        kT = A(16384, [128, NLAT + 256], BF16)
        Vaug = A(20992, [128, 18, 2, 128], BF16)
        Vaug2 = A(20992, [128, 36, 128], BF16)
        qcT = A(30208, [128, 2, 4, 128], BF16)
        kcT = A(32256, [128, 256], BF16)
        kTp = [A(92160, [128, NLAT + 256], BF16), A(96768, [128, NLAT + 256], BF16)]
        kcTp = [A(101376, [128, 256], BF16), A(101888, [128, 256], BF16)]
        cVaug = A(32768, [128, 2, 2, 128], BF16)
        cVaug2 = A(32768, [128, 4, 128], BF16)
        Uc = A(33792, [128, 2, 512], BF16)
        mixT = A(35840, [128, 8, NTOK], BF16)
        w_in_sb = A(35840, [128, 8, 1280], BF16)
        hTb = [A(56320, [128, 8, 512], BF16), A(78848, [128, 8, 512], BF16)]
        rope = A(64512, [128, 2, NLAT], BF16)
        Utm = A(72704, [128, 4, 512], BF16)
        praw = [A(76800, [128, 512], BF16), A(77824, [128, 512], BF16), A(91136, [128, 512], BF16)]
        rtmp = [A(87040 + 2048 * i, [128, 512], F32) for i in range(2)]
        masks = A(72704, [128, 4, 512], BF16)
        esink = A(76800, [128, 2, 512], F32)
        NPT = 6
        PT = [A(80896 + 1024 * i, [128, 512], BF16) for i in range(NPT)]
        den = A(87040, [128, 512], F32)
        rec = A(89088, [128, 512], F32)
        UR = A(0, [64, 64, 128], BF16)
        URf = A(0, [128, 64, 128], BF16)
        G = A(16384, [128, 64, 64], BF16)
        Y2 = A(72704, [128, 128, 64], BF16)
        PQ = A(89088, [128, 2, NLAT], BF16)
        Wf_sb = A(97280, [64, 8, 64], F32)
        Mx = A(101376, [128, 4, 2, 128], BF16)
        CS256 = A(105472, [128, 2, 2, 256], BF16)
        PQc = A(107520, [128, 2, 256], BF16)
        w_out_sb = A(0, [128, 8, 1024], BF16)
        mixfb = [A(16384, [128, 8, 512], F32), A(72704, [128, 8, 512], F32)]
        ring = [A(RING_BYTES * i, [128, RING_BYTES // 2], BF16) for i in range(3)]
        hfb = [A(105472, [128, 8, 768], BF16), A(33792, [128, 8, 768], BF16)]
        act = A(46080, [128, 22, 768], BF16)
        fout = A(79872, [128, 8, 768], F32)

        def w8(slot):
            return ring[slot][:, 0:4096].rearrange("p (a b) -> p a b", a=8)

        def wd(slot):
            return ring[slot][:, 0:5632].rearrange("p (a b) -> p a b", a=22)

        st = {"bank": 0, "ring": 0, "alt": 0}

        def nb():
            b = st["bank"]
            st["bank"] = (b + 1) % 7
            return b

        def nring():
            r = st["ring"]
            st["ring"] = (r + 1) % 3
            return r

        def evac_eng():
            st["alt"] ^= 1
            return "scalar" if st["alt"] else "vector"

        def CP(eng, out, in_, reads, writes):
            if eng == "scalar":
                op("scalar", lambda e: e.copy(out=out, in_=in_), reads=reads, writes=writes)
            else:
                op("vector", lambda e: e.tensor_copy(out=out, in_=in_), reads=reads, writes=writes)

        def MM(out, lhsT, rhs, start, stop):
            return lambda e: e.matmul(out, lhsT=lhsT, rhs=rhs, start=start, stop=stop)

        def TT(out, in0, in1, alu, reads, writes):
            op("vector", lambda e: e.tensor_tensor(out=out, in0=in0, in1=in1, op=alu), reads=reads, writes=writes)

        def STT(out, in0, scalar, in1, op0, op1, reads, writes):
            op("vector", lambda e: e.scalar_tensor_tensor(out=out, in0=in0, scalar=scalar, in1=in1, op0=op0, op1=op1),
               reads=reads, writes=writes)

        def ACT(out, in_, func, reads, writes, scale=None, bias=None):
            kw = {}
            if scale is not None:
                kw["scale"] = scale
            if bias is not None:
                kw["bias"] = bias
            op("scalar", lambda e: e.activation(out=out, in_=in_, func=func, **kw), reads=reads, writes=writes)

        def DM(eng, out, in_, tag, reads=(), writes=()):
            dma(eng, lambda e: e.dma_start(out=out, in_=in_), tag=tag, reads=reads, writes=writes)

        def xres(c0, T):
            return [("xT", b) for b in range(c0 // 128, (c0 + T + 127) // 128)]

        def mm_group(out, pairs, reads, bank):
            n = len(pairs)
            fns = [(lambda e, l=l, r=r, i=i: e.matmul(out, lhsT=l, rhs=r, start=(i == 0), stop=(i == n - 1)))
                   for i, (l, r) in enumerate(pairs)]
            op("tensor", fns, reads=reads, writes=[("ps", bank)])

        op("vector", lambda e: e.memset(ones_mean[:], 1.0 / 1024.0), writes=["ones_mean"])
        op("vector", lambda e: e.memset(epsc[:], EPS), writes=["epsc"])
        for i, (dst, src, nm) in enumerate([
            (permT[:], perm_d[:, :], "permT"), (F1[:], F1_d[:, :], "F1"),
            (CS64[:].rearrange("p a b -> p (a b)"), CS64_d[:, :], "CS64"),
            (cvec[:].rearrange("p a b -> p (a b)"), cvec_d[:, :], "cvec"),
            (b_adaT[:].rearrange("p a b -> p (a b)"), b_ada_d[:, :], "b_adaT"),
            (gains[:].rearrange("p a b c -> p (a b c)"), gains_d[:, :], "gains"),
        ]):
            DM("sync", dst, src, ("c", i), writes=[nm])
        for t in (0, 3):
            DM("sync", xT[:, :, t * 512:(t + 1) * 512],
               xT_d[:, :, t * 512:(t + 1) * 512].rearrange("k p n -> p k n"), ("x", t), writes=xres(t * 512, 512))

        def stats_rstd(T):
            mm_group(ps[7][:, 0:T], [(ones_mean[:], sqb[:, kc, 0:T]) for kc in range(8)], ["sqb", "ones_mean"], 7)
            ACT(rstd[:, 0:T], ps[7][:, 0:T], AF.Ln, [("ps", 7), "epsc"], ["rstd"], bias=epsc[:, 0:1])
            ACT(rstd[:, 0:T], rstd[:, 0:T], AF.Exp, ["rstd"], ["rstd"], scale=-0.5)

        def prenorm(c0, T, col, Ai, Boff, dest, dest_res):
            ACT(sqb[:, :, 0:T], xT[:, :, c0:c0 + T], AF.Square, xres(c0, T), ["sqb"])
            stats_rstd(T)
            for kc in range(8):
                t = tmp[kc % 2]
                TT(t[:, 0:T], xT[:, kc, c0:c0 + T], rstd[:, 0:T], ALU.mult, xres(c0, T) + ["rstd"], [("tmp", kc % 2)])
                ACT(dest(kc), t[:, 0:T], AF.Identity, [("tmp", kc % 2), "vec", "modT"], [dest_res],
                    scale=vec[:, col, Ai, kc:kc + 1], bias=modT[:, Boff + kc, col:col + 1])

        def postnorm(src, src_res, c0, T, col, Gi):
            ACT(sqb[:, :, 0:T], src, AF.Square, [src_res], ["sqb"])
            stats_rstd(T)
            for kc in range(8):
                t = tmp[kc % 2]
                TT(t[:, 0:T], src[:, kc, :], rstd[:, 0:T], ALU.mult, [src_res, "rstd"], [("tmp", kc % 2)])
                STT(xT[:, kc, c0:c0 + T], t[:, 0:T], vec[:, col, Gi, kc:kc + 1], xT[:, kc, c0:c0 + T],
                    ALU.mult, ALU.add, [("tmp", kc % 2), "vec"] + xres(c0, T), xres(c0, T))

        def stats_items(src_of, reads, T):
            items = []
            def stat_mm(kc):
                op("tensor", MM(ps[7][:, 0:T], ones_mean[:], sqb[:, kc, 0:T], kc == 0, kc == 7),
                   reads=[("sqb", kc), "ones_mean"], writes=[("ps", 7)])

            for kc in range(8):
                def it(kc=kc):
                    ACT(sqb[:, kc, 0:T], src_of(kc), AF.Square, reads, [("sqb", kc)])
                    if kc > 0:
                        stat_mm(kc - 1)
                items.append(it)

            def fin():
                stat_mm(7)
                ACT(rstd[:, 0:T], ps[7][:, 0:T], AF.Ln, [("ps", 7), "epsc"], ["rstd"], bias=epsc[:, 0:1])
                ACT(rstd[:, 0:T], rstd[:, 0:T], AF.Exp, ["rstd"], ["rstd"], scale=-0.5)
            items.append(fin)
            return items

        def prenorm_items(c0, T, col, Ai, Boff, dest, dest_res):
            items = stats_items(lambda kc: xT[:, kc, c0:c0 + T], xres(c0, T), T)
            for kc in range(8):
                def it(kc=kc):
                    t = tmp[kc % 2]
                    TT(t[:, 0:T], xT[:, kc, c0:c0 + T], rstd[:, 0:T], ALU.mult, xres(c0, T) + ["rstd"],
                       [("tmp", kc % 2)])
                    ACT(dest(kc), t[:, 0:T], AF.Identity, [("tmp", kc % 2), "vec", "modT"], [dest_res],
                        scale=vec[:, col, Ai, kc:kc + 1], bias=modT[:, Boff + kc, col:col + 1])
                items.append(it)
            return items

        def postnorm_items(src, src_res, c0, T, col, Gi):
            items = stats_items(lambda kc: src[:, kc, :], [src_res], T)
            for kc in range(8):
                def it(kc=kc):
                    t = tmp[kc % 2]
                    TT(t[:, 0:T], src[:, kc, :], rstd[:, 0:T], ALU.mult, [src_res, "rstd"], [("tmp", kc % 2)])
                    STT(xT[:, kc, c0:c0 + T], t[:, 0:T], vec[:, col, Gi, kc:kc + 1], xT[:, kc, c0:c0 + T],
                        ALU.mult, ALU.add, [("tmp", kc % 2), "vec"] + xres(c0, T), xres(c0, T))
                items.append(it)
            return items

        bgq = []

        def bg(n=1):
            for _ in range(n):
                if bgq:
                    bgq.pop(0)()

        def bg_drain():
            while bgq:
                bgq.pop(0)()

        def wview(w, L):
            return w[L].rearrange("(kc p) n -> p kc n", p=128)

        r4 = lambda ap: ap.rearrange("p (a b) -> p a b", a=4)

        ACT(scb[:], cvec[:], AF.Silu, ["cvec"], ["scb"])
        for l_ in range(1):
            for bk in range(6):
                slot = nring()
                DM("gpsimd", w8(slot), wview(w_ada_d, l_)[:, :, bk * 512:(bk + 1) * 512], ("ringp", slot),
                   writes=[("ring", slot)])
                for s_ in range(4):
                    c2 = (l_ * 24 + bk * 4 + s_) * 2
                    mm_group(ps[7][:, c2:c2 + 2],
                             [(w8(slot)[:, kc, s_ * 128:(s_ + 1) * 128], scb[:, kc, :]) for kc in range(8)],
                             [("ring", slot), "scb"], 7)
        CP("vector", mpart[:], ps[7][:, 0:48], [("ps", 7)], ["mpart"])
        DM("sync", payM[:, :], mpart[:], ("c", 20), reads=["mpart"], writes=["payM"])
        pin_, pout_ = payM.ap().opt(), gathM.ap().opt()
        dma("gpsimd", lambda e: e.collective_compute(
            "AllGather", ALU.bypass, replica_groups=[[0, 1], [2, 3], [4, 5], [6, 7]], ins=[pin_], outs=[pout_]),
            tag=("ccM",), reads=["payM"], writes=["gathM"], inc=1)
        DM("sync", modall[:], gathM[:, :].rearrange("(r p) c -> p r c", p=128), ("c", 21), reads=["gathM"],
           writes=["modall"])
        for t in (1, 2):
            DM("sync", xT[:, :, t * 512:(t + 1) * 512],
               xT_d[:, :, t * 512:(t + 1) * 512].rearrange("k p n -> p k n"), ("x", t), writes=xres(t * 512, 512))
        DM("sync", xT[:, :, NLAT:NTOK], xcT_d[:, :, :].rearrange("k p n -> p k n"), ("x", 4), writes=xres(NLAT, NCTX))
        S.barrier(exclude=[("D", ("x", 1)), ("D", ("x", 2)), ("D", ("x", 4))])

        stopped = False

        def stop_at(name):
            return _STOP == name

        for L in range(2):
            first = (L == 0)
            if stopped:
                break
            if first:
                ML = modall[:].rearrange("p r (o j) -> p r o j", o=24)
                for col in range(2):
                    TT(modT[:, :, col].rearrange("p (r o) -> p r o", r=2), ML[:, :, :, col],
                       b_adaT[:, L, :].rearrange("p (r o) -> p r o", r=2), ALU.add, ["modall", "b_adaT"], ["modT"])
            else:
                for col in range(2):
                    TT(modT[:, :, col], m1stage[:].rearrange("p (o j) -> p o j", j=2)[:, :, col], b_adaT[:, L, :],
                       ALU.add, ["m1stage", "b_adaT"], ["modT"])
            for col in range(2):
                for idx, (lo, gi, plus1) in enumerate([(8, 0, True), (16, 1, False), (32, 2, True), (40, 3, False)]):
                    if plus1:
                        STT(vec[:, col, idx, :], modT[:, lo:lo + 8, col], 1.0, gains[:, L, gi, :], ALU.add, ALU.mult,
                            ["modT", "gains"], ["vec"])
                    else:
                        TT(vec[:, col, idx, :], modT[:, lo:lo + 8, col], gains[:, L, gi, :], ALU.mult,
                           ["modT", "gains"], ["vec"])
            S.barrier()
            if stop_at(f"mod{L}"):
                stopped = True
                break

            for i, (c0, c1) in enumerate([(0, 512), (512, 1024), (1024, 1280)]):
                DM("gpsimd", w_in_sb[:, :, c0:c1], wview(w_in_d, L)[:, :, c0:c1], ("win", i), writes=[("w_in_sb", i)])
            WIN = [("w_in_sb", i) for i in range(3)]
            DM("sync", rope[:].rearrange("p a b -> p (a b)"), rope_d[:, :], ("c", 10), writes=["rope"])
            op("vector", lambda e: e.memset(Vaug2[:], 1.0), writes=["Vaug"])
            op("vector", lambda e: e.memset(cVaug2[:], 1.0), writes=["cVaug"])
            payw, payEw = [], []
            m1tiles = [(t * 512, 512, 0) for t in (0, 3, 1, 2)] + [(NLAT, NCTX, 1)]

            def m1_pre(ti):
                c0_, T_, col_ = m1tiles[ti]
                hb = hTb[ti % 2]
                return prenorm_items(c0_, T_, col_, 0, 0, lambda kc, hb=hb, T_=T_: hb[:, kc, 0:T_], ("hT", ti % 2))

            bgq.extend(m1_pre(0))
            bg_drain()
            for ti, (c0, T, col) in enumerate(m1tiles):
                hT = hTb[ti % 2]
                HR = ("hT", ti % 2)
                if ti + 1 < len(m1tiles):
                    bgq.extend(m1_pre(ti + 1))
                if col == 0:
                    t = c0 // 512
                    def rope_stage(oc):
                        pr = praw[oc % 3]
                        b2 = nb()
                        mm_group(ps[b2][:, :], [(permT[:], pr[:])], [("praw", oc % 3), "permT"], b2)
                        TT(rtmp[0][:], pr[:], rope[:, 0, c0:c0 + 512], ALU.mult, [("praw", oc % 3), "rope"],
                           [("rtmp", 0)])
                        TT(rtmp[1][:], ps[b2][:, :], rope[:, 1, c0:c0 + 512], ALU.mult, [("ps", b2), "rope"],
                           [("rtmp", 1)])
                        if oc < 4:
                            dest, dres = qT[:, 4 * t:4 * t + 4, oc, :], "qT"
                        else:
                            dest, dres = r4(kT[:, c0:c0 + 512]), "kT"
                        TT(dest, r4(rtmp[0][:]), r4(rtmp[1][:]), ALU.add, [("rtmp", 0), ("rtmp", 1)], [dres])

                    for oc in range(5):
                        b = nb()
                        mm_group(ps[b][:, :],
                                 [(w_in_sb[:, kc, oc * 128:(oc + 1) * 128], hT[:, kc, :]) for kc in range(8)],
                                 [HR] + WIN, b)
                        CP("scalar", praw[oc % 3][:], ps[b][:, :], [("ps", b)], [("praw", oc % 3)])
                        if oc > 0:
                            rope_stage(oc - 1)
                        bg(2)
                    b = nb()
                    for blk in range(4):
                        mm_group(ps[b][:, blk * 128:(blk + 1) * 128],
                                 [(hT[:, kc, blk * 128:(blk + 1) * 128], w_in_sb[:, kc, 640:768]) for kc in range(8)],
                                 [HR] + WIN, b)
                    CP("scalar", Vaug2[:, 8 * t:8 * t + 8, 0:64], ps[b][:, :].rearrange("p (a d) -> p a d", d=64),
                       [("ps", b)], ["Vaug"])
                    rope_stage(4)
                    bg(2)
                    for blk in range(4):
                        b = nb()
                        mm_group(ps[b][:, :],
                                 [(hT[:, kc, blk * 128:(blk + 1) * 128], w_in_sb[:, kc, 768:1280]) for kc in range(8)],
                                 [HR] + WIN, b)
                        CP(evac_eng(), Utm[:, blk, :], ps[b][:, :], [("ps", b)], [("Utm", blk)])
                        bg(2)
                    DM("sync", pay[L][0:8192, :].rearrange("(t j) c -> t (j c)", j=4)[c0:c0 + 512, :]
                       .rearrange("(b p) c -> p b c", p=128), Utm[:], ("payw", L, t),
                       reads=[("Utm", k) for k in range(4)], writes=[("pay", L, t)])
                    payw.append(("pay", L, t))
                else:
                    for oc in ([0, 1, 2, 3, 4] if first else [4]):
                        b = nb()
                        mm_group(ps[b][:, 0:NCTX],
                                 [(w_in_sb[:, kc, oc * 128:(oc + 1) * 128], hT[:, kc, 0:NCTX]) for kc in range(8)],
                                 [HR] + WIN, b)
                        if oc < 4:
                            CP("scalar", qcT[:, :, oc, :], ps[b][:, 0:NCTX].rearrange("p (a b) -> p a b", a=2),
                               [("ps", b)], ["qcT"])
                        else:
                            CP("scalar", kcT[:], ps[b][:, 0:NCTX], [("ps", b)], ["kcT"])
                    b = nb()
                    for blk in range(2):
                        mm_group(ps[b][:, blk * 128:(blk + 1) * 128],
                                 [(hT[:, kc, blk * 128:(blk + 1) * 128], w_in_sb[:, kc, 640:768]) for kc in range(8)],
                                 [HR] + WIN, b)
                    CP("scalar", cVaug2[:, :, 0:64], ps[b][:, 0:256].rearrange("p (a d) -> p a d", d=64),
                       [("ps", b)], ["cVaug"])
                    if first:
                        for blk in range(2):
                            b = nb()
                            mm_group(ps[b][:, :],
                                     [(hT[:, kc, blk * 128:(blk + 1) * 128], w_in_sb[:, kc, 768:1280])
                                      for kc in range(8)], [HR] + WIN, b)
                            CP(evac_eng(), Uc[:, blk, :], ps[b][:, :], [("ps", b)], ["Uc"])
                bg_drain()
                if ti == 1:
                    for e_i, (cs, blk) in enumerate([(0, 0), (NLAT - 128, 15)]):
                        DM("sync", payE[L][e_i * 128:(e_i + 1) * 128, :], kT[:, cs:cs + 128], ("payEw", L, e_i),
                           reads=["kT"], writes=[("pay", L, 10 + e_i)])
                        DM("sync", payE[L][256 + e_i * 128:256 + (e_i + 1) * 128, :].rearrange("p (k d) -> p k d", k=2),
                           Vaug[:, blk, :, 0:64], ("payEw", L, 2 + e_i), reads=["Vaug"], writes=[("pay", L, 20 + e_i)])
                        payEw += [("pay", L, 10 + e_i), ("pay", L, 20 + e_i)]
                    if "cc" not in _SKIP:
                        pinE, poutE = payE[L].ap().opt(), gathE[L].ap().opt()
                        dma("gpsimd", lambda e, pinE=pinE, poutE=poutE: e.collective_compute(
                            "AllGather", ALU.bypass, replica_groups=[[0, 1], [2, 3], [4, 5], [6, 7]], ins=[pinE],
                            outs=[poutE]), tag=("ccE", L), reads=payEw, writes=[("gathE", L)], inc=1)
            if "cc" not in _SKIP:
                pin, pout = pay[L].ap().opt(), gath[L].ap().opt()
                dma("gpsimd", lambda e, pin=pin, pout=pout: e.collective_compute(
                    "AllGather", ALU.bypass, replica_groups=[[0, 1], [2, 3], [4, 5], [6, 7]], ins=[pin], outs=[pout]),
                    tag=("ccU", L), reads=payw, writes=[("gath", L)], inc=1)
            S.barrier(exclude=[("D", ("ccU", L))])
            if stop_at(f"M1{L}"):
                stopped = True
                break

            for bk in range(6):
                ncol = 512 if bk < 5 else 256
                DM("gpsimd", wg_bf[L][bk][:, :, 0:ncol], wview(w_gate_d, L)[:, :, bk * 512:bk * 512 + ncol],
                   ("pc", L), reads=[("gath", L)], writes=[("wgbf", L, bk)])
                DM("gpsimd", wu_bf[L][bk][:, :, 0:ncol], wview(w_up_d, L)[:, :, bk * 512:bk * 512 + ncol],
                   ("pc", L), writes=[("wubf", L, bk)])
            for bk in range(4):
                DM("gpsimd", wd_bf[L][bk], w_down_d[L].rearrange("(j p) n -> p j n", p=128)[:, :, bk * 256:(bk + 1) * 256],
                   ("pc", L), writes=[("wdbf", L, bk)])
            DM("sync", masks[:].rearrange("p a b -> p (a b)"), masks_d[:, :], ("c", 11), writes=["masks"])
            DM("sync", esink[:].rearrange("p a b -> p (a b)"), sink_d[:, L * 1024:(L + 1) * 1024], ("c", 12),
               writes=["esink"])
            ACT(esink[:], esink[:], AF.Exp, ["esink"], ["esink"])
            for h_i, (rk, ed) in enumerate([(0, 1), (1, 0)]):
                r0 = rk * 512 + ed * 128
                DM("sync", kT[:, NLAT + h_i * 128:NLAT + (h_i + 1) * 128], gathE[L][r0:r0 + 128, :],
                   ("halo", L, h_i), reads=[("gathE", L)], writes=["kT"])
                r1 = rk * 512 + 256 + ed * 128
                DM("sync", Vaug[:, 16 + h_i, :, 0:64], gathE[L][r1:r1 + 128, :].rearrange("p (k d) -> p k d", k=2),
                   ("halo", L, 2 + h_i), reads=[("gathE", L)], writes=["Vaug"])

            for kh_ in range(2):
                z = slice((1 - kh_) * 64, (2 - kh_) * 64)
                c = slice(kh_ * 64, (kh_ + 1) * 64)
                za, zb = kTp[kh_][z, :], kcTp[kh_][z, :]
                op("vector", lambda e, za=za: e.memset(za, 0.0), writes=[("kTp", kh_)])
                op("vector", lambda e, zb=zb: e.memset(zb, 0.0), writes=[("kcTp", kh_)])
                CP("vector" if kh_ == 0 else "scalar", kTp[kh_][c, :], kT[c, :], ["kT"], [("kTp", kh_)])
                CP("scalar" if kh_ == 0 else "vector", kcTp[kh_][c, :], kcT[c, :], ["kcT"], [("kcTp", kh_)])
            units = []
            for n in range(16):
                for kh in range(2):
                    pr_ = slice(kh * 64, (kh + 1) * 64)
                    blocks = []
                    if n > 0:
                        blocks.append((kTp[kh][:, (n - 1) * 128:n * 128], Vaug[:, n - 1, kh, :], 0))
                    else:
                        blocks.append((kTp[kh][:, NLAT:NLAT + 128], Vaug[:, 16, kh, :], 2))
                    blocks.append((kTp[kh][:, n * 128:(n + 1) * 128], Vaug[:, n, kh, :], None))
                    if n < 15:
                        blocks.append((kTp[kh][:, (n + 1) * 128:(n + 2) * 128], Vaug[:, n + 1, kh, :], 1))
                    else:
                        blocks.append((kTp[kh][:, NLAT + 128:NLAT + 256], Vaug[:, 17, kh, :], 3))
                    blocks.append((kcTp[kh][:, 0:128], cVaug[:, 0, kh, :], None))
                    blocks.append((kcTp[kh][:, 128:256], cVaug[:, 1, kh, :], None))
                    units.append((qT[:, n, :, :].rearrange("p a b -> p (a b)"), blocks,
                                  mixT[pr_, 0:4, n * 128:(n + 1) * 128], kh))
            if first:
                for nq in range(2):
                    for kh in range(2):
                        pr_ = slice(kh * 64, (kh + 1) * 64)
                        blocks = [(kcTp[kh][:, 0:128], cVaug[:, 0, kh, :], None),
                                  (kcTp[kh][:, 128:256], cVaug[:, 1, kh, :], None)]
                        units.append((qcT[:, nq, :, :].rearrange("p a b -> p (a b)"), blocks,
                                      mixT[pr_, 0:4, NLAT + nq * 128:NLAT + (nq + 1) * 128], kh))
            steps = []
            for u, (qsrc, blocks, dest, kh) in enumerate(units):
                for i, (K, V, m) in enumerate(blocks):
                    steps.append((u, i, len(blocks), qsrc, K, V, m, dest, kh))
            LOOK = 3
            accs = {}
            for k in range(len(steps) + LOOK):
                if k < len(steps):
                    (u, i, nblk, qsrc, K, V, m, dest, kh) = steps[k]
                    sbk = k % 4
                    mm_group(ps[sbk][:, :], [(K, qsrc)], ["qT", ("kTp", kh), "qcT", ("kcTp", kh)], sbk)
                    pt = PT[k % NPT]
                    ACT(pt[:], ps[sbk][:, :], AF.Exp, [("ps", sbk)], [("PT", k % NPT)], scale=0.125)
                    if m is not None:
                        TT(pt[:], pt[:], masks[:, m, :], ALU.mult, [("PT", k % NPT), "masks"], [("PT", k % NPT)])
                if k >= LOOK:
                    kk = k - LOOK
                    (u, i, nblk, qsrc, K, V, m, dest, kh) = steps[kk]
                    if i == 0:
                        accs[u] = 4 + u % 3
                    acc = accs[u]
                    pt = PT[kk % NPT]
                    op("tensor", MM(ps[acc][:, :], V, pt[:], i == 0, i == nblk - 1),
                       reads=[("PT", kk % NPT), "Vaug", "cVaug"], writes=[("ps", acc)])
                    if i == nblk - 1:
                        TT(den[64:128, :], ps[acc][64:128, :], esink[64:128, kh, :], ALU.add,
                           [("ps", acc), "esink"], ["den"])
                        ACT(den[64:128, :], den[64:128, :], AF.Ln, ["den"], ["den"])
                        ACT(rec[0:64, :], den[64:128, :], AF.Exp, ["den"], ["rec"], scale=-1.0)
                        TT(dest, r4(ps[acc][0:64, :]), r4(rec[0:64, :]), ALU.mult, [("ps", acc), "rec"], ["mixT"])
            S.barrier(exclude=[("D", ("pc", L))])
            if stop_at(f"M2{L}"):
                stopped = True
                break

            DM("sync", G[:].rearrange("p a b -> p (a b)"), G_d[:, :], ("c", 13), writes=["G"])
            DM("sync", Wf_sb[:], w_f_d[L].rearrange("g c d -> c g d"), ("c", 14), writes=["Wf"])
            if first:
                DM("sync", CS256[:].rearrange("p a b c -> p (a b c)"), CS256_d[:, :], ("c", 15), writes=["CS256"])
            mxflat = Mx[:].rearrange("p a b c -> p (a b c)")
            op("vector", lambda e, mxflat=mxflat: e.memset(mxflat, 0.0), writes=["Mx"])
            urz = URf[64:128, :, :].rearrange("p a b -> p (a b)")
            op("scalar", lambda e, urz=urz: e.activation(out=urz, in_=urz, func=AF.Copy, scale=0.0), writes=["URz"])
            for g in range(8):
                for pq in range(2):
                    b = nb()
                    mm_group(ps[b][0:64, 0:64], [(CS64[:, pq, :], Wf_sb[:, g, :])], ["CS64", "Wf"], b)
                    o = (g % 2) * 64
                    CP("vector", Mx[o:o + 64, g // 2, pq, o:o + 64], ps[b][0:64, 0:64], [("ps", b)], ["Mx"])
            for j in range(4):
                for rk in range(2):
                    DM("sync", UR[rk * 32:(rk + 1) * 32, :, :],
                       gath[L][rk * 8192:(rk + 1) * 8192, :]
                       .rearrange("(rl c jj) e -> rl c (jj e)", c=64, jj=4)[:, :, j * 128:(j + 1) * 128],
                       ("UR", rk), reads=[("gath", L)], writes=[("UR", rk)])
                for q4 in range(32):
                    b = nb()
                    fns = [MM(ps[b][0:64, s_ * 128:(s_ + 1) * 128], URf[:, :, q4 * 4 + s_], F1[:, :], True, True)
                           for s_ in range(4)]
                    op("tensor", fns, reads=[("UR", 0), ("UR", 1), "F1", "URz"], writes=[("ps", b)])
                    src4 = ps[b][0:64, :].rearrange("p (a t k) -> p a t k", a=4, t=2)
                    ev = "scalar" if q4 % 2 == 0 else "vector"
                    CP(ev, Y2[0:64, q4 * 4:(q4 + 1) * 4, :], src4[:, :, 0, :], [("ps", b)], [("Y2", ev)])
                    CP(ev, Y2[64:128, q4 * 4:(q4 + 1) * 4, :], src4[:, :, 1, :], [("ps", b)], [("Y2", ev)])
                for k8 in range(8):
                    b = nb()
                    fns = [MM(ps[b][:, kk * 64:(kk + 1) * 64], Y2[:, :, k8 * 8 + kk], G[:, k8 * 8 + kk, :], True, True)
                           for kk in range(8)]
                    op("tensor", fns, reads=[("Y2", "scalar"), ("Y2", "vector"), "G"], writes=[("ps", b)])
                    for pq in range(2):
                        outv = PQ[:, pq, :].rearrange("p (k2 k1) -> p k2 k1", k1=64)[:, :, k8 * 8:(k8 + 1) * 8]
                        inv = ps[b][:, :].rearrange("p (kk pq k2) -> p pq k2 kk", kk=8, pq=2)[:, pq, :, :]
                        ev2 = "scalar" if k8 % 2 == 0 else "vector"
                        CP(ev2, outv, inv, [("ps", b)], [("PQ", ev2)])
                for t in range(4):
                    b = nb()
                    mm_group(ps[b][:, :], [(Mx[:, j, 0, :], PQ[:, 0, t * 512:(t + 1) * 512]),
                                           (Mx[:, j, 1, :], PQ[:, 1, t * 512:(t + 1) * 512])],
                             [("PQ", "scalar"), ("PQ", "vector"), "Mx"], b)
                    CP(evac_eng(), mixT[:, 4 + j, t * 512:(t + 1) * 512], ps[b][:, :], [("ps", b)], ["mixT"])
                if first:
                    for pq in range(2):
                        b = nb()
                        mm_group(ps[b][:, 0:256],
                                 [(Uc[:, kc2, j * 128:(j + 1) * 128], CS256[:, pq, kc2, :]) for kc2 in range(2)],
                                 ["Uc", "CS256"], b)
                        CP(evac_eng(), PQc[:, pq, :], ps[b][:, 0:256], [("ps", b)], ["PQc"])
                    b = nb()
                    mm_group(ps[b][:, 0:256], [(Mx[:, j, 0, :], PQc[:, 0, :]), (Mx[:, j, 1, :], PQc[:, 1, :])],
                             ["PQc", "Mx"], b)
                    CP(evac_eng(), mixT[:, 4 + j, NLAT:NTOK], ps[b][:, 0:256], [("ps", b)], ["mixT"])
            S.barrier(exclude=[("D", ("pc", L))])
            if stop_at(f"M3{L}"):
                stopped = True
                break

            for i in range(2):
                DM("gpsimd", w_out_sb[:, :, i * 512:(i + 1) * 512], wview(w_out_d, L)[:, :, i * 512:(i + 1) * 512],
                   ("wout", i), writes=[("w_out_sb", i)])
            if first:
                passes = [[(0, 384, 0, 0), (384, 384, 0, 384)], [(768, 384, 0, 0), (1152, 384, 0, 384)],
                          [(1536, 384, 0, 0), (1920, 128, 0, 384), (2048, 256, 1, 512)]]
            else:
                passes = [[(0, 384, 0, 0), (384, 384, 0, 384)], [(768, 384, 0, 0), (1152, 384, 0, 384)],
                          [(1536, 256, 0, 0), (1792, 256, 0, 256)]]
            def ffn_pre(p):
                items = []
                hfp = hfb[p % 2]
                for (c0_, T_, col_, loc_) in passes[p]:
                    items += prenorm_items(c0_, T_, col_, 2, 24,
                                           lambda kc, loc_=loc_, T_=T_, hfp=hfp: hfp[:, kc, loc_:loc_ + T_], ("hf", p % 2))
                return items

            tiles4 = [(t * 512, 512, 0) for t in range(4)] + ([(NLAT, NCTX, 1)] if first else [])
            m1_blocks = []
            if first:
                m2s = [A(89088 + 8192 * i_, [128, 8, 512], BF16) for i_ in range(2)]
                m1_blocks = list(range(12))

                def m1_load(bk):
                    DM("gpsimd", m2s[bk % 2], wview(w_ada1_d, 0)[:, :, bk * 512:(bk + 1) * 512], ("m2s", bk % 2),
                       writes=[("m2s", bk % 2)])

                for bk in range(2):
                    m1_load(bk)

            def m1_block():
                bk_ = m1_blocks.pop(0)
                b_ = nb()
                for s_ in range(4):
                    mm_group(ps[b_][:, s_ * 2:s_ * 2 + 2],
                             [(m2s[bk_ % 2][:, kc, s_ * 128:(s_ + 1) * 128], scb[:, kc, :]) for kc in range(8)],
                             [("m2s", bk_ % 2), "scb"], b_)
                CP("vector", m1stage[:, bk_ * 8:(bk_ + 1) * 8], ps[b_][:, 0:8], [("ps", b_)], ["m1stage"])
                if bk_ + 2 < 12:
                    m1_load(bk_ + 2)
            for ti, (c0, T, col) in enumerate(tiles4):
                mixf = mixfb[ti % 2]
                for oc in range(8):
                    b = nb()
                    mm_group(ps[b][:, 0:T],
                             [(w_out_sb[:, kc, oc * 128:(oc + 1) * 128], mixT[:, kc, c0:c0 + T]) for kc in range(8)],
                             ["mixT", ("w_out_sb", oc // 4)], b)
                    CP("scalar", mixf[:, oc, 0:T], ps[b][:, 0:T], [("ps", b)], [("mixf", ti % 2)])
                    bg(3)
                    if m1_blocks and (ti * 8 + oc) % 3 == 2:
                        m1_block()
                bg_drain()
                bgq.extend(postnorm_items(mixf[:, :, 0:T], ("mixf", ti % 2), c0, T, col, 1))
                if ti == 2:
                    bgq.extend(ffn_pre(0))
            bg_drain()
            while m1_blocks:
                m1_block()
            S.barrier()
            if stop_at(f"M4{L}"):
                stopped = True
                break

            bg_drain()
            for p, tiles in enumerate(passes):
                hf = hfb[p % 2]
                HF = ("hf", p % 2)
                for bk in range(6):
                    ncol = 512 if bk < 5 else 256
                    sa = nring()
                    DM("sync", w8(sa)[:, :, 0:ncol], wg_bf[L][bk][:, :, 0:ncol],
                       ("ring", sa), reads=[("wgbf", L, bk)], writes=[("ring", sa)])
                    su = nring()
                    DM("sync", w8(su)[:, :, 0:ncol], wu_bf[L][bk][:, :, 0:ncol],
                       ("ring", su), reads=[("wubf", L, bk)], writes=[("ring", su)])
                    for s_ in range(ncol // 128):
                        jj = bk * 4 + s_
                        for (c0, T, col, loc) in tiles:
                            bg_ = nb()
                            mm_group(ps[bg_][:, 0:T],
                                     [(w8(sa)[:, kc, s_ * 128:(s_ + 1) * 128], hf[:, kc, loc:loc + T]) for kc in range(8)],
                                     [HF, ("ring", sa)], bg_)
                            bu = nb()
                            mm_group(ps[bu][:, 0:T],
                                     [(w8(su)[:, kc, s_ * 128:(s_ + 1) * 128], hf[:, kc, loc:loc + T]) for kc in range(8)],
                                     [HF, ("ring", su)], bu)
                            ti = jj % 2
                            ACT(tmp[ti][:, 0:T], ps[bg_][:, 0:T], AF.Silu, [("ps", bg_)], [("tmp", ti)])
                            TT(act[:, jj, loc:loc + T], tmp[ti][:, 0:T], ps[bu][:, 0:T], ALU.mult,
                               [("tmp", ti), ("ps", bu)], ["act"])
                            bg(1)
                bg_drain()
                if p + 1 < len(passes):
                    bgq.extend(ffn_pre(p + 1))
                for bk in range(4):
                    sd = nring()
                    DM("sync", wd(sd), wd_bf[L][bk], ("ring", sd), reads=[("wdbf", L, bk)], writes=[("ring", sd)])
                    for s_ in range(2):
                        oc = bk * 2 + s_
                        for (c0, T, col, loc) in tiles:
                            b = nb()
                            mm_group(ps[b][:, 0:T],
                                     [(wd(sd)[:, j2, s_ * 128:(s_ + 1) * 128], act[:, j2, loc:loc + T]) for j2 in range(22)],
                                     ["act", ("ring", sd)], b)
                            CP("scalar", fout[:, oc, loc:loc + T], ps[b][:, 0:T], [("ps", b)], ["fout"])
                            bg(3)
                bg_drain()
                for (c0, T, col, loc) in tiles:
                    bgq.extend(postnorm_items(fout[:, :, loc:loc + T], "fout", c0, T, col, 3))
            bg_drain()
            S.barrier()

        for t in range(4):
            DM("sync", outT_d[:, :, t * 512:(t + 1) * 512].rearrange("k p n -> p k n"), xT[:, :, t * 512:(t + 1) * 512],
               "out", reads=xres(t * 512, 512), writes=[("out", t)])
        S._waits("sync", [S.latest[("D", "out")]])

        sems = {}
        for i, k in enumerate(S.all_semkeys()):
            sems[k] = es.enter_context(nc.semaphore(f"s{i}"))
        with nc.Block() as block:
            @block.tensor
            def _(e):
                S.emit("tensor", e, sems)

            @block.vector
            def _(e):
                S.emit("vector", e, sems)

            @block.scalar
            def _(e):
                S.emit("scalar", e, sems)

            @block.gpsimd
            def _(e):
                S.emit("gpsimd", e, sems)

            @block.sync
            def _(e):
                S.emit("sync", e, sems)
    return nc


def _tables(half):
    t = np.arange(NLAT) + half * NLAT
    row = (t // 64).astype(np.float32)
    col = (t % 64).astype(np.float32)
    freqs = (np.float32(10000.0) ** (-np.arange(16, dtype=np.float32) / np.float32(16))).astype(np.float32)
    ang = np.concatenate([row[:, None] * freqs, col[:, None] * freqs], axis=-1).astype(np.float32)
    cos, sin = np.cos(ang), np.sin(ang)
    d = np.arange(128) % 64
    pair = d // 2
    ropeC = cos[:, pair].T
    sgn = np.where(d % 2 == 0, -1.0, 1.0)[:, None]
    ropeS = sin[:, pair].T * sgn
    rope = np.concatenate([ropeC, ropeS], axis=1).astype(NPBF)
    kj = np.arange(128)[:, None]
    qi = np.arange(128)[None, :]
    ML = (qi <= kj).astype(np.float32)
    MR = (kj <= qi).astype(np.float32)
    ms = [ML, MR, ML * (1.0 if half == 1 else 0.0), MR * (1.0 if half == 0 else 0.0)]
    masks = np.concatenate([np.tile(m, (1, 4)) for m in ms], axis=1).astype(NPBF)
    c = np.arange(64, dtype=np.float64)[:, None, None]
    k1 = np.arange(64, dtype=np.float64)[None, :, None]
    k2 = (half * 32 + np.arange(32, dtype=np.float64))[None, None, :]
    th = 2 * np.pi * c * (k1 + 64 * k2) / 4096.0
    Gr = np.concatenate([np.cos(th), np.sin(th)], axis=2) / 512.0
    Gi = np.concatenate([np.sin(th), -np.cos(th)], axis=2) / 512.0
    G = np.concatenate([Gr.reshape(64, -1), Gi.reshape(64, -1)], axis=0).astype(NPBF)
    return rope, masks, G


def kernel(x, c, ctx, c_ctx, w_ada, b_ada, norm_pre_mix, norm_post_mix, norm_pre_ffn, norm_post_ffn,
           w_in, w_out, w_fourier, sink, w_gate, w_up, w_down):
    f = lambda a: np.ascontiguousarray(np.asarray(a, dtype=np.float32))
    x, c, ctx, c_ctx = f(x), f(c), f(ctx), f(c_ctx)
    w_ada, b_ada, w_in, w_out = f(w_ada), f(b_ada), f(w_in), f(w_out)
    w_fourier, sink, w_gate, w_up, w_down = f(w_fourier), f(sink), f(w_gate), f(w_up), f(w_down)
    gs = [f(norm_pre_mix), f(norm_post_mix), f(norm_pre_ffn), f(norm_post_ffn)]
    qperm = np.concatenate([np.r_[cc * 64:(cc + 1) * 64, (4 + cc) * 64:(5 + cc) * 64] for cc in range(4)])
    w_in_p = np.ascontiguousarray(np.concatenate([w_in[:, :, qperm], w_in[:, :, 512:]], axis=2))
    w_out_p = np.ascontiguousarray(np.concatenate([w_out[:, qperm, :], w_out[:, 512:, :]], axis=1))
    b_adaT = np.ascontiguousarray(b_ada.reshape(2, 48, 128).transpose(2, 0, 1).reshape(128, 96))
    gains = np.stack(gs, axis=1)
    gains = np.ascontiguousarray(gains.reshape(2, 4, 8, 128).transpose(3, 0, 1, 2).reshape(128, 64))
    sinkb = np.ascontiguousarray(
        np.broadcast_to(sink.reshape(1, 2, 2, 4, 1), (128, 2, 2, 4, 128)).reshape(128, 2048))
    permT = np.zeros((128, 128), np.float32)
    permT[np.arange(128), np.arange(128) ^ 1] = 1.0
    permT = permT.astype(NPBF)
    r = np.arange(64, dtype=np.float64)
    ph = 2 * np.pi * np.outer(r, r) / 64.0
    F1 = np.concatenate([np.cos(ph), -np.sin(ph)], axis=1)
    F1 = np.concatenate([F1, F1], axis=0).astype(NPBF)
    CS64 = np.concatenate([np.cos(ph), -np.sin(ph)], axis=1).astype(np.float32)
    n256 = np.arange(256, dtype=np.float64)
    ph2 = 2 * np.pi * np.outer(n256, n256) / 256.0
    cs = np.stack([np.cos(ph2), np.sin(ph2)], axis=0) / 128.0
    CS256 = np.ascontiguousarray(cs.reshape(2, 2, 128, 256).transpose(2, 0, 1, 3).reshape(128, -1)).astype(NPBF)
    tabs = [_tables(0), _tables(1)]
    w_ada_halves = [np.ascontiguousarray(w_ada[0:1, :, h * 3072:(h + 1) * 3072]) for h in range(2)]
    w_ada1 = np.ascontiguousarray(w_ada[1:2])

    in_maps = []
    for core in range(8):
        b, half = core // 2, core % 2
        xs = x[b, half * NLAT:(half + 1) * NLAT, :]
        xT = np.ascontiguousarray(xs.T.reshape(8, 128, NLAT))
        xcT = np.ascontiguousarray(ctx[b].T.reshape(8, 128, NCTX))
        cv = np.stack([c[b], c_ctx], axis=1)
        cvec = np.ascontiguousarray(cv.reshape(8, 128, 2).transpose(1, 0, 2).reshape(128, 16))
        w_adas = w_ada_halves[half]
        rope, masks, G = tabs[half]
        in_maps.append({
            "xT": xT, "xcT": xcT, "cvec": cvec, "w_adas": w_adas, "w_ada1": w_ada1, "b_adaT": b_adaT,
            "gains": gains,
            "w_in": w_in_p, "w_out": w_out_p, "w_four": w_fourier, "sinkb": sinkb,
            "w_gate": w_gate, "w_up": w_up, "w_down": w_down,
            "permT": permT, "rope": rope, "masks": masks, "F1": F1, "G": G, "CS64": CS64, "CS256": CS256,
        })
    nc = build_program()
    res = run_bass_kernel_spmd(nc, in_maps, core_ids=list(range(8)))
    out = np.empty((4, 4096, 1024), np.float32)
    for core in range(8):
        b, half = core // 2, core % 2
        oT = np.asarray(res.results[core]["outT"]).reshape(1024, NLAT)
        out[b, half * NLAT:(half + 1) * NLAT, :] = oT.T
    return out
```

```python
import contextlib
import numpy as np
import ml_dtypes
import concourse.bass as bass
import concourse.mybir as mybir
from concourse.bass_utils import run_bass_kernel_spmd

F32 = mybir.dt.float32
BF16 = mybir.dt.bfloat16
ALU = mybir.AluOpType
AF = mybir.ActivationFunctionType
NPBF = ml_dtypes.bfloat16

ENGS = ["tensor", "vector", "scalar", "gpsimd", "sync"]
CH = 4000
EPS = 1e-6
NLAT = 2048
NCTX = 256
NTOK = NLAT + NCTX
ARENA_BYTES = 116736
RING_BYTES = 11264


class Sched:
    def __init__(self):
        self.stream = {e: [] for e in ENGS}
        self.cnt = {e: 0 for e in ENGS}
        self.last_w = {}
        self.readers = {}
        self.waited = {e: {} for e in ENGS}
        self.dma_cnt = {}
        self.latest = {}

    def _deps(self, reads, writes):
        deps = []
        for r in reads:
            t = self.last_w.get(r)
            if t:
                deps.append(t)
        for w in writes:
            t = self.last_w.get(w)
            if t:
                deps.append(t)
            for k, n in self.readers.get(w, {}).items():
                deps.append((k[0], k[1], n))
        return deps

    def _waits(self, eng, deps):
        best = {}
        for kind, name, n in deps:
            if eng == "tensor" and kind == "E" and name == "tensor":
                continue
            key = (kind, name)
            best[key] = max(best.get(key, 0), n)
        for key, n in best.items():
            if self.waited[eng].get(key, 0) >= n:
                continue
            self.waited[eng][key] = n
            self.stream[eng].append(("wait", (key[0], key[1], n)))

    def _commit(self, tok, reads, writes):
        key = (tok[0], tok[1])
        for r in reads:
            d = self.readers.setdefault(r, {})
            d[key] = max(d.get(key, 0), tok[2])
        for w in writes:
            self.last_w[w] = tok
            self.readers[w] = {}
        self.latest[key] = tok

    def op(self, eng, fns, reads=(), writes=()):
        if callable(fns):
            fns = [fns]
        self._waits(eng, self._deps(reads, writes))
        for fn in fns:
            self.cnt[eng] += 1
            self.stream[eng].append(("ins", fn, ("E", eng, self.cnt[eng])))
        self._commit(("E", eng, self.cnt[eng]), reads, writes)

    def dma(self, eng, fn, tag, reads=(), writes=(), inc=16):
        self._waits(eng, self._deps(reads, writes))
        n = self.dma_cnt.get(tag, 0) + inc
        self.dma_cnt[tag] = n
        tok = ("D", tag, n)
        self.stream[eng].append(("dma", fn, tok, inc))
        self._commit(tok, reads, writes)

    def barrier(self, exclude=()):
        toks = [t for k, t in self.latest.items() if k not in exclude]
        for e in ENGS:
            self._waits(e, toks)

    @staticmethod
    def semkey(tok):
        kind, name, n = tok
        if kind == "E":
            return ("E", name, (n - 1) // CH), (n - 1) % CH + 1
        return ("D", name), n

    def all_semkeys(self):
        keys = []
        for e in ENGS:
            for k in range((self.cnt[e] + CH - 1) // CH):
                keys.append(("E", e, k))
        for tag in self.dma_cnt:
            keys.append(("D", tag))
        return keys

    def emit(self, eng, handle, sems):
        for item in self.stream[eng]:
            if item[0] == "wait":
                k, v = self.semkey(item[1])
                handle.wait_ge(sems[k], v)
            elif item[0] == "ins":
                k, v = self.semkey(item[2])
                item[1](handle).then_inc(sems[k], 1)
            else:
                k, v = self.semkey(item[2])
                if item[3] == 16:
                    item[1](handle).then_inc(sems[k], 16)
                else:
                    item[1](handle).then_inc(sems[k])


_STOP = None
_SKIP = set()


def build_program():
    nc = bass.Bass("TRN2", target_bir_lowering=False)
    S = Sched()
    op, dma = S.op, S.dma

    def din(name, shape, dt=F32):
        return nc.dram_tensor(name, shape, dt, kind="ExternalInput").ap()

    xT_d = din("xT", [8, 128, NLAT])
    xcT_d = din("xcT", [8, 128, NCTX])
    cvec_d = din("cvec", [128, 16])
    w_ada_d = din("w_adas", [1, 1024, 3072])
    w_ada1_d = din("w_ada1", [1, 1024, 6144])
    b_ada_d = din("b_adaT", [128, 96])
    gains_d = din("gains", [128, 64])
    w_in_d = din("w_in", [2, 1024, 1280])
    w_out_d = din("w_out", [2, 1024, 1024])
    w_f_d = din("w_four", [2, 8, 64, 64])
    sink_d = din("sinkb", [128, 2 * 2 * 512])
    w_gate_d = din("w_gate", [2, 1024, 2816])
    w_up_d = din("w_up", [2, 1024, 2816])
    w_down_d = din("w_down", [2, 2816, 1024])
    perm_d = din("permT", [128, 128], BF16)
    rope_d = din("rope", [128, 2 * NLAT], BF16)
    masks_d = din("masks", [128, 4 * 512], BF16)
    F1_d = din("F1", [128, 128], BF16)
    G_d = din("G", [128, 64 * 64], BF16)
    CS64_d = din("CS64", [64, 128], F32)
    CS256_d = din("CS256", [128, 2 * 2 * 256], BF16)
    outT_d = nc.dram_tensor("outT", [8, 128, NLAT], F32, kind="ExternalOutput").ap()
    pay = [nc.dram_tensor(f"pay{i}", [8192, 128], BF16) for i in range(2)]
    gath = [nc.dram_tensor(f"gath{i}", [16384, 128], BF16) for i in range(2)]
    wg_bf = [nc.dram_tensor(f"wgbf{i}", [6, 128, 8, 512], BF16) for i in range(2)]
    wu_bf = [nc.dram_tensor(f"wubf{i}", [6, 128, 8, 512], BF16) for i in range(2)]
    wd_bf = [nc.dram_tensor(f"wdbf{i}", [4, 128, 22, 256], BF16) for i in range(2)]
    payM = nc.dram_tensor("payM", [128, 48], F32)
    gathM = nc.dram_tensor("gathM", [256, 48], F32)
    payE = [nc.dram_tensor(f"payE{i}", [512, 128], BF16) for i in range(2)]
    gathE = [nc.dram_tensor(f"gathE{i}", [1024, 128], BF16) for i in range(2)]

    es = contextlib.ExitStack()
    with es:
        es.enter_context(nc.allow_low_precision("bf16 matmul operands, fp32 accumulation"))
        es.enter_context(nc.allow_non_contiguous_dma("strided layouts"))

        def sb(name, shape, dt):
            return es.enter_context(nc.sbuf_tensor(name, shape, dt))

        xT = sb("xT_sb", [128, 8, NTOK], F32)
        ones_mean = sb("ones_mean", [128, 128], BF16)
        permT = sb("permT_sb", [128, 128], BF16)
        F1 = sb("F1_sb", [128, 128], BF16)
        CS64 = sb("CS64_sb", [64, 2, 64], F32)
        cvec = sb("cvec_sb", [128, 8, 2], F32)
        scb = sb("scb", [128, 8, 2], BF16)
        mpart = sb("mpart", [128, 48], F32)
        modall = sb("modall", [128, 2, 48], F32)
        m1stage = sb("m1stage", [128, 96], F32)
        b_adaT = sb("b_adaT_sb", [128, 2, 48], F32)
        gains = sb("gains_sb", [128, 2, 4, 8], F32)
        modT = sb("modT", [128, 48, 2], F32)
        vec = sb("vec", [128, 2, 4, 8], F32)
        epsc = sb("epsc", [128, 1], F32)
        sqb = sb("sqb", [128, 8, 512], BF16)
        rstd = sb("rstd", [128, 512], F32)
        tmp = [sb(f"tmp{i}", [128, 512], F32) for i in range(2)]
        arena = sb("arena", [128, ARENA_BYTES // 2], BF16)
        ps = [es.enter_context(nc.psum_tensor(f"ps{i}", [128, 512], F32)) for i in range(8)]

        def A(off, shape, dt):
            esz = 2 if dt == BF16 else 4
            nbytes = int(np.prod(shape[1:])) * esz
            assert off % 4 == 0 and off + nbytes <= ARENA_BYTES, (off, shape)
            v = arena[0:shape[0], off // 2:(off + nbytes) // 2]
            if dt == F32:
                v = v.bitcast(F32)
            if len(shape) == 2:
                return v
            names = "abcd"[:len(shape) - 1]
            pat = "p (" + " ".join(names) + ") -> p " + " ".join(names)
            return v.rearrange(pat, **{n: s for n, s in zip(names, shape[1:])})

        qT = A(0, [128, 16, 4, 128], BF16)
        kT = A(16384, [128, NLAT + 256], BF16)
        Vaug = A(20992, [128, 18, 2, 128], BF16)
        Vaug2 = A(20992, [128, 36, 128], BF16)
        qcT = A(30208, [128, 2, 4, 128], BF16)
        kcT = A(32256, [128, 256], BF16)
        kTp = [A(92160, [128, NLAT + 256], BF16), A(96768, [128, NLAT + 256], BF16)]
        kcTp = [A(101376, [128, 256], BF16), A(101888, [128, 256], BF16)]
        cVaug = A(32768, [128, 2, 2, 128], BF16)
        cVaug2 = A(32768, [128, 4, 128], BF16)
        Uc = A(33792, [128, 2, 512], BF16)
        mixT = A(35840, [128, 8, NTOK], BF16)
        w_in_sb = A(35840, [128, 8, 1280], BF16)
        hTb = [A(56320, [128, 8, 512], BF16), A(78848, [128, 8, 512], BF16)]
        rope = A(64512, [128, 2, NLAT], BF16)
        Utm = A(72704, [128, 4, 512], BF16)
        praw = [A(76800, [128, 512], BF16), A(77824, [128, 512], BF16), A(91136, [128, 512], BF16)]
        rtmp = [A(87040 + 2048 * i, [128, 512], F32) for i in range(2)]
        masks = A(72704, [128, 4, 512], BF16)
        esink = A(76800, [128, 2, 512], F32)
        NPT = 6
        PT = [A(80896 + 1024 * i, [128, 512], BF16) for i in range(NPT)]
        den = A(87040, [128, 512], F32)
        rec = A(89088, [128, 512], F32)
        UR = A(0, [64, 64, 128], BF16)
        URf = A(0, [128, 64, 128], BF16)
        G = A(16384, [128, 64, 64], BF16)
        Y2 = A(72704, [128, 128, 64], BF16)
        PQ = A(89088, [128, 2, NLAT], BF16)
        Wf_sb = A(97280, [64, 8, 64], F32)
        Mx = A(101376, [128, 4, 2, 128], BF16)
        CS256 = A(105472, [128, 2, 2, 256], BF16)
        PQc = A(107520, [128, 2, 256], BF16)
        w_out_sb = A(0, [128, 8, 1024], BF16)
        mixfb = [A(16384, [128, 8, 512], F32), A(72704, [128, 8, 512], F32)]
        ring = [A(RING_BYTES * i, [128, RING_BYTES // 2], BF16) for i in range(3)]
        hfb = [A(33792, [128, 8, 768], BF16), A(46080, [128, 8, 768], BF16)]
        act = A(58368, [128, 22, 768], BF16)
        fout = A(92160, [128, 8, 768], F32)

        def w8(slot):
            return ring[slot][:, 0:4096].rearrange("p (a b) -> p a b", a=8)

        def wd(slot):
            return ring[slot][:, 0:5632].rearrange("p (a b) -> p a b", a=22)

        st = {"bank": 0, "ring": 0, "alt": 0}

        def nb():
            b = st["bank"]
            st["bank"] = (b + 1) % 7
            return b

        def nring():
            r = st["ring"]
            st["ring"] = (r + 1) % 3
            return r

        def evac_eng():
            st["alt"] ^= 1
            return "scalar" if st["alt"] else "vector"

        def CP(eng, out, in_, reads, writes):
            if eng == "scalar":
                op("scalar", lambda e: e.copy(out=out, in_=in_), reads=reads, writes=writes)
            else:
                op("vector", lambda e: e.tensor_copy(out=out, in_=in_), reads=reads, writes=writes)

        def MM(out, lhsT, rhs, start, stop):
            return lambda e: e.matmul(out, lhsT=lhsT, rhs=rhs, start=start, stop=stop)

        def TT(out, in0, in1, alu, reads, writes):
            op("vector", lambda e: e.tensor_tensor(out=out, in0=in0, in1=in1, op=alu), reads=reads, writes=writes)

        def STT(out, in0, scalar, in1, op0, op1, reads, writes):
            op("vector", lambda e: e.scalar_tensor_tensor(out=out, in0=in0, scalar=scalar, in1=in1, op0=op0, op1=op1),
               reads=reads, writes=writes)

        def ACT(out, in_, func, reads, writes, scale=None, bias=None):
            kw = {}
            if scale is not None:
                kw["scale"] = scale
            if bias is not None:
                kw["bias"] = bias
            op("scalar", lambda e: e.activation(out=out, in_=in_, func=func, **kw), reads=reads, writes=writes)

        def DM(eng, out, in_, tag, reads=(), writes=()):
            dma(eng, lambda e: e.dma_start(out=out, in_=in_), tag=tag, reads=reads, writes=writes)

        def xres(c0, T):
            return [("xT", b) for b in range(c0 // 128, (c0 + T + 127) // 128)]

        def mm_group(out, pairs, reads, bank):
            n = len(pairs)
            fns = [(lambda e, l=l, r=r, i=i: e.matmul(out, lhsT=l, rhs=r, start=(i == 0), stop=(i == n - 1)))
                   for i, (l, r) in enumerate(pairs)]
            op("tensor", fns, reads=reads, writes=[("ps", bank)])

        op("vector", lambda e: e.memset(ones_mean[:], 1.0 / 1024.0), writes=["ones_mean"])
        op("vector", lambda e: e.memset(epsc[:], EPS), writes=["epsc"])
        for i, (dst, src, nm) in enumerate([
            (permT[:], perm_d[:, :], "permT"), (F1[:], F1_d[:, :], "F1"),
            (CS64[:].rearrange("p a b -> p (a b)"), CS64_d[:, :], "CS64"),
            (cvec[:].rearrange("p a b -> p (a b)"), cvec_d[:, :], "cvec"),
            (b_adaT[:].rearrange("p a b -> p (a b)"), b_ada_d[:, :], "b_adaT"),
            (gains[:].rearrange("p a b c -> p (a b c)"), gains_d[:, :], "gains"),
        ]):
            DM("sync", dst, src, ("c", i), writes=[nm])
        for t in (0, 3):
            DM("sync", xT[:, :, t * 512:(t + 1) * 512],
               xT_d[:, :, t * 512:(t + 1) * 512].rearrange("k p n -> p k n"), ("x", t), writes=xres(t * 512, 512))

        def stats_rstd(T):
            mm_group(ps[7][:, 0:T], [(ones_mean[:], sqb[:, kc, 0:T]) for kc in range(8)], ["sqb", "ones_mean"], 7)
            ACT(rstd[:, 0:T], ps[7][:, 0:T], AF.Ln, [("ps", 7), "epsc"], ["rstd"], bias=epsc[:, 0:1])
            ACT(rstd[:, 0:T], rstd[:, 0:T], AF.Exp, ["rstd"], ["rstd"], scale=-0.5)

        def prenorm(c0, T, col, Ai, Boff, dest, dest_res):
            ACT(sqb[:, :, 0:T], xT[:, :, c0:c0 + T], AF.Square, xres(c0, T), ["sqb"])
            stats_rstd(T)
            for kc in range(8):
                t = tmp[kc % 2]
                TT(t[:, 0:T], xT[:, kc, c0:c0 + T], rstd[:, 0:T], ALU.mult, xres(c0, T) + ["rstd"], [("tmp", kc % 2)])
                ACT(dest(kc), t[:, 0:T], AF.Identity, [("tmp", kc % 2), "vec", "modT"], [dest_res],
                    scale=vec[:, col, Ai, kc:kc + 1], bias=modT[:, Boff + kc, col:col + 1])

        def postnorm(src, src_res, c0, T, col, Gi):
            ACT(sqb[:, :, 0:T], src, AF.Square, [src_res], ["sqb"])
            stats_rstd(T)
            for kc in range(8):
                t = tmp[kc % 2]
                TT(t[:, 0:T], src[:, kc, :], rstd[:, 0:T], ALU.mult, [src_res, "rstd"], [("tmp", kc % 2)])
                STT(xT[:, kc, c0:c0 + T], t[:, 0:T], vec[:, col, Gi, kc:kc + 1], xT[:, kc, c0:c0 + T],
                    ALU.mult, ALU.add, [("tmp", kc % 2), "vec"] + xres(c0, T), xres(c0, T))

        def stats_items(src_of, reads, T):
            items = []
            def stat_mm(kc):
                op("tensor", MM(ps[7][:, 0:T], ones_mean[:], sqb[:, kc, 0:T], kc == 0, kc == 7),
                   reads=[("sqb", kc), "ones_mean"], writes=[("ps", 7)])

            for kc in range(8):
                def it(kc=kc):
                    ACT(sqb[:, kc, 0:T], src_of(kc), AF.Square, reads, [("sqb", kc)])
                    if kc > 0:
                        stat_mm(kc - 1)
                items.append(it)

            def fin():
                stat_mm(7)
                ACT(rstd[:, 0:T], ps[7][:, 0:T], AF.Ln, [("ps", 7), "epsc"], ["rstd"], bias=epsc[:, 0:1])
                ACT(rstd[:, 0:T], rstd[:, 0:T], AF.Exp, ["rstd"], ["rstd"], scale=-0.5)
            items.append(fin)
            return items

        def prenorm_items(c0, T, col, Ai, Boff, dest, dest_res):
            items = stats_items(lambda kc: xT[:, kc, c0:c0 + T], xres(c0, T), T)
            for kc in range(8):
                def it(kc=kc):
                    t = tmp[kc % 2]
                    TT(t[:, 0:T], xT[:, kc, c0:c0 + T], rstd[:, 0:T], ALU.mult, xres(c0, T) + ["rstd"],
                       [("tmp", kc % 2)])
                    ACT(dest(kc), t[:, 0:T], AF.Identity, [("tmp", kc % 2), "vec", "modT"], [dest_res],
                        scale=vec[:, col, Ai, kc:kc + 1], bias=modT[:, Boff + kc, col:col + 1])
                items.append(it)
            return items

        def postnorm_items(src, src_res, c0, T, col, Gi):
            items = stats_items(lambda kc: src[:, kc, :], [src_res], T)
            for kc in range(8):
                def it(kc=kc):
                    t = tmp[kc % 2]
                    TT(t[:, 0:T], src[:, kc, :], rstd[:, 0:T], ALU.mult, [src_res, "rstd"], [("tmp", kc % 2)])
                    STT(xT[:, kc, c0:c0 + T], t[:, 0:T], vec[:, col, Gi, kc:kc + 1], xT[:, kc, c0:c0 + T],
                        ALU.mult, ALU.add, [("tmp", kc % 2), "vec"] + xres(c0, T), xres(c0, T))
                items.append(it)
            return items

        bgq = []

        def bg(n=1):
            for _ in range(n):
                if bgq:
                    bgq.pop(0)()

        def bg_drain():
            while bgq:
                bgq.pop(0)()

        def wview(w, L):
            return w[L].rearrange("(kc p) n -> p kc n", p=128)

        r4 = lambda ap: ap.rearrange("p (a b) -> p a b", a=4)

        ACT(scb[:], cvec[:], AF.Silu, ["cvec"], ["scb"])
        for l_ in range(1):
            for bk in range(6):
                slot = nring()
                DM("gpsimd", w8(slot), wview(w_ada_d, l_)[:, :, bk * 512:(bk + 1) * 512], ("ringp", slot),
                   writes=[("ring", slot)])
                for s_ in range(4):
                    c2 = (l_ * 24 + bk * 4 + s_) * 2
                    mm_group(ps[7][:, c2:c2 + 2],
                             [(w8(slot)[:, kc, s_ * 128:(s_ + 1) * 128], scb[:, kc, :]) for kc in range(8)],
                             [("ring", slot), "scb"], 7)
        CP("vector", mpart[:], ps[7][:, 0:48], [("ps", 7)], ["mpart"])
        DM("sync", payM[:, :], mpart[:], ("c", 20), reads=["mpart"], writes=["payM"])
        pin_, pout_ = payM.ap().opt(), gathM.ap().opt()
        dma("gpsimd", lambda e: e.collective_compute(
            "AllGather", ALU.bypass, replica_groups=[[0, 1], [2, 3], [4, 5], [6, 7]], ins=[pin_], outs=[pout_]),
            tag=("ccM",), reads=["payM"], writes=["gathM"], inc=1)
        DM("sync", modall[:], gathM[:, :].rearrange("(r p) c -> p r c", p=128), ("c", 21), reads=["gathM"],
           writes=["modall"])
        for t in (1, 2):
            DM("sync", xT[:, :, t * 512:(t + 1) * 512],
               xT_d[:, :, t * 512:(t + 1) * 512].rearrange("k p n -> p k n"), ("x", t), writes=xres(t * 512, 512))
        DM("sync", xT[:, :, NLAT:NTOK], xcT_d[:, :, :].rearrange("k p n -> p k n"), ("x", 4), writes=xres(NLAT, NCTX))
        S.barrier(exclude=[("D", ("x", 1)), ("D", ("x", 2)), ("D", ("x", 4))])

        stopped = False

        def stop_at(name):
            return _STOP == name

        for L in range(2):
            first = (L == 0)
            if stopped:
                break
            if first:
                ML = modall[:].rearrange("p r (o j) -> p r o j", o=24)
                for col in range(2):
                    TT(modT[:, :, col].rearrange("p (r o) -> p r o", r=2), ML[:, :, :, col],
                       b_adaT[:, L, :].rearrange("p (r o) -> p r o", r=2), ALU.add, ["modall", "b_adaT"], ["modT"])
            else:
                for col in range(2):
                    TT(modT[:, :, col], m1stage[:].rearrange("p (o j) -> p o j", j=2)[:, :, col], b_adaT[:, L, :],
                       ALU.add, ["m1stage", "b_adaT"], ["modT"])
            for col in range(2):
                for idx, (lo, gi, plus1) in enumerate([(8, 0, True), (16, 1, False), (32, 2, True), (40, 3, False)]):
                    if plus1:
                        STT(vec[:, col, idx, :], modT[:, lo:lo + 8, col], 1.0, gains[:, L, gi, :], ALU.add, ALU.mult,
                            ["modT", "gains"], ["vec"])
                    else:
                        TT(vec[:, col, idx, :], modT[:, lo:lo + 8, col], gains[:, L, gi, :], ALU.mult,
                           ["modT", "gains"], ["vec"])
            S.barrier()
            if stop_at(f"mod{L}"):
                stopped = True
                break

            for i, (c0, c1) in enumerate([(0, 512), (512, 1024), (1024, 1280)]):
                DM("gpsimd", w_in_sb[:, :, c0:c1], wview(w_in_d, L)[:, :, c0:c1], ("win", i), writes=[("w_in_sb", i)])
            WIN = [("w_in_sb", i) for i in range(3)]
            DM("sync", rope[:].rearrange("p a b -> p (a b)"), rope_d[:, :], ("c", 10), writes=["rope"])
            op("vector", lambda e: e.memset(Vaug2[:], 1.0), writes=["Vaug"])
            op("vector", lambda e: e.memset(cVaug2[:], 1.0), writes=["cVaug"])
            payw, payEw = [], []
            m1tiles = [(t * 512, 512, 0) for t in (0, 3, 1, 2)] + [(NLAT, NCTX, 1)]

            def m1_pre(ti):
                c0_, T_, col_ = m1tiles[ti]
                hb = hTb[ti % 2]
                return prenorm_items(c0_, T_, col_, 0, 0, lambda kc, hb=hb, T_=T_: hb[:, kc, 0:T_], ("hT", ti % 2))

            bgq.extend(m1_pre(0))
            bg_drain()
            for ti, (c0, T, col) in enumerate(m1tiles):
                hT = hTb[ti % 2]
                HR = ("hT", ti % 2)
                if ti + 1 < len(m1tiles):
                    bgq.extend(m1_pre(ti + 1))
                if col == 0:
                    t = c0 // 512
                    def rope_stage(oc):
                        pr = praw[oc % 3]
                        b2 = nb()
                        mm_group(ps[b2][:, :], [(permT[:], pr[:])], [("praw", oc % 3), "permT"], b2)
                        TT(rtmp[0][:], pr[:], rope[:, 0, c0:c0 + 512], ALU.mult, [("praw", oc % 3), "rope"],
                           [("rtmp", 0)])
                        TT(rtmp[1][:], ps[b2][:, :], rope[:, 1, c0:c0 + 512], ALU.mult, [("ps", b2), "rope"],
                           [("rtmp", 1)])
                        if oc < 4:
                            dest, dres = qT[:, 4 * t:4 * t + 4, oc, :], "qT"
                        else:
                            dest, dres = r4(kT[:, c0:c0 + 512]), "kT"
                        TT(dest, r4(rtmp[0][:]), r4(rtmp[1][:]), ALU.add, [("rtmp", 0), ("rtmp", 1)], [dres])

                    for oc in range(5):
                        b = nb()
                        mm_group(ps[b][:, :],
                                 [(w_in_sb[:, kc, oc * 128:(oc + 1) * 128], hT[:, kc, :]) for kc in range(8)],
                                 [HR] + WIN, b)
                        CP("scalar", praw[oc % 3][:], ps[b][:, :], [("ps", b)], [("praw", oc % 3)])
                        if oc > 0:
                            rope_stage(oc - 1)
                        bg(2)
                    b = nb()
                    for blk in range(4):
                        mm_group(ps[b][:, blk * 128:(blk + 1) * 128],
                                 [(hT[:, kc, blk * 128:(blk + 1) * 128], w_in_sb[:, kc, 640:768]) for kc in range(8)],
                                 [HR] + WIN, b)
                    CP("scalar", Vaug2[:, 8 * t:8 * t + 8, 0:64], ps[b][:, :].rearrange("p (a d) -> p a d", d=64),
                       [("ps", b)], ["Vaug"])
                    rope_stage(4)
                    bg(2)
                    for blk in range(4):
                        b = nb()
                        mm_group(ps[b][:, :],
                                 [(hT[:, kc, blk * 128:(blk + 1) * 128], w_in_sb[:, kc, 768:1280]) for kc in range(8)],
                                 [HR] + WIN, b)
                        CP(evac_eng(), Utm[:, blk, :], ps[b][:, :], [("ps", b)], [("Utm", blk)])
                        bg(2)
                    DM("sync", pay[L][0:8192, :].rearrange("(t j) c -> t (j c)", j=4)[c0:c0 + 512, :]
                       .rearrange("(b p) c -> p b c", p=128), Utm[:], ("payw", L, t),
                       reads=[("Utm", k) for k in range(4)], writes=[("pay", L, t)])
                    payw.append(("pay", L, t))
                else:
                    for oc in ([0, 1, 2, 3, 4] if first else [4]):
                        b = nb()
                        mm_group(ps[b][:, 0:NCTX],
                                 [(w_in_sb[:, kc, oc * 128:(oc + 1) * 128], hT[:, kc, 0:NCTX]) for kc in range(8)],
                                 [HR] + WIN, b)
                        if oc < 4:
                            CP("scalar", qcT[:, :, oc, :], ps[b][:, 0:NCTX].rearrange("p (a b) -> p a b", a=2),
                               [("ps", b)], ["qcT"])
                        else:
                            CP("scalar", kcT[:], ps[b][:, 0:NCTX], [("ps", b)], ["kcT"])
                    b = nb()
                    for blk in range(2):
                        mm_group(ps[b][:, blk * 128:(blk + 1) * 128],
                                 [(hT[:, kc, blk * 128:(blk + 1) * 128], w_in_sb[:, kc, 640:768]) for kc in range(8)],
                                 [HR] + WIN, b)
                    CP("scalar", cVaug2[:, :, 0:64], ps[b][:, 0:256].rearrange("p (a d) -> p a d", d=64),
                       [("ps", b)], ["cVaug"])
                    if first:
                        for blk in range(2):
                            b = nb()
                            mm_group(ps[b][:, :],
                                     [(hT[:, kc, blk * 128:(blk + 1) * 128], w_in_sb[:, kc, 768:1280])
                                      for kc in range(8)], [HR] + WIN, b)
                            CP(evac_eng(), Uc[:, blk, :], ps[b][:, :], [("ps", b)], ["Uc"])
                bg_drain()
                if ti == 1:
                    for e_i, (cs, blk) in enumerate([(0, 0), (NLAT - 128, 15)]):
                        DM("sync", payE[L][e_i * 128:(e_i + 1) * 128, :], kT[:, cs:cs + 128], ("payEw", L, e_i),
                           reads=["kT"], writes=[("pay", L, 10 + e_i)])
                        DM("sync", payE[L][256 + e_i * 128:256 + (e_i + 1) * 128, :].rearrange("p (k d) -> p k d", k=2),
                           Vaug[:, blk, :, 0:64], ("payEw", L, 2 + e_i), reads=["Vaug"], writes=[("pay", L, 20 + e_i)])
                        payEw += [("pay", L, 10 + e_i), ("pay", L, 20 + e_i)]
                    if "cc" not in _SKIP:
                        pinE, poutE = payE[L].ap().opt(), gathE[L].ap().opt()
                        dma("gpsimd", lambda e, pinE=pinE, poutE=poutE: e.collective_compute(
                            "AllGather", ALU.bypass, replica_groups=[[0, 1], [2, 3], [4, 5], [6, 7]], ins=[pinE],
                            outs=[poutE]), tag=("ccE", L), reads=payEw, writes=[("gathE", L)], inc=1)
            if "cc" not in _SKIP:
                pin, pout = pay[L].ap().opt(), gath[L].ap().opt()
                dma("gpsimd", lambda e, pin=pin, pout=pout: e.collective_compute(
                    "AllGather", ALU.bypass, replica_groups=[[0, 1], [2, 3], [4, 5], [6, 7]], ins=[pin], outs=[pout]),
                    tag=("ccU", L), reads=payw, writes=[("gath", L)], inc=1)
            S.barrier(exclude=[("D", ("ccU", L))])
            if stop_at(f"M1{L}"):
                stopped = True
                break

            for bk in range(6):
                ncol = 512 if bk < 5 else 256
                DM("gpsimd", wg_bf[L][bk][:, :, 0:ncol], wview(w_gate_d, L)[:, :, bk * 512:bk * 512 + ncol],
                   ("pc", L), reads=[("gath", L)], writes=[("wgbf", L, bk)])
                DM("gpsimd", wu_bf[L][bk][:, :, 0:ncol], wview(w_up_d, L)[:, :, bk * 512:bk * 512 + ncol],
                   ("pc", L), writes=[("wubf", L, bk)])
            for bk in range(4):
                DM("gpsimd", wd_bf[L][bk], w_down_d[L].rearrange("(j p) n -> p j n", p=128)[:, :, bk * 256:(bk + 1) * 256],
                   ("pc", L), writes=[("wdbf", L, bk)])
            DM("sync", masks[:].rearrange("p a b -> p (a b)"), masks_d[:, :], ("c", 11), writes=["masks"])
            DM("sync", esink[:].rearrange("p a b -> p (a b)"), sink_d[:, L * 1024:(L + 1) * 1024], ("c", 12),
               writes=["esink"])
            ACT(esink[:], esink[:], AF.Exp, ["esink"], ["esink"])
            for h_i, (rk, ed) in enumerate([(0, 1), (1, 0)]):
                r0 = rk * 512 + ed * 128
                DM("sync", kT[:, NLAT + h_i * 128:NLAT + (h_i + 1) * 128], gathE[L][r0:r0 + 128, :],
                   ("halo", L, h_i), reads=[("gathE", L)], writes=["kT"])
                r1 = rk * 512 + 256 + ed * 128
                DM("sync", Vaug[:, 16 + h_i, :, 0:64], gathE[L][r1:r1 + 128, :].rearrange("p (k d) -> p k d", k=2),
                   ("halo", L, 2 + h_i), reads=[("gathE", L)], writes=["Vaug"])

            for kh_ in range(2):
                z = slice((1 - kh_) * 64, (2 - kh_) * 64)
                c = slice(kh_ * 64, (kh_ + 1) * 64)
                za, zb = kTp[kh_][z, :], kcTp[kh_][z, :]
                op("vector", lambda e, za=za: e.memset(za, 0.0), writes=[("kTp", kh_)])
                op("vector", lambda e, zb=zb: e.memset(zb, 0.0), writes=[("kcTp", kh_)])
                CP("vector" if kh_ == 0 else "scalar", kTp[kh_][c, :], kT[c, :], ["kT"], [("kTp", kh_)])
                CP("scalar" if kh_ == 0 else "vector", kcTp[kh_][c, :], kcT[c, :], ["kcT"], [("kcTp", kh_)])
            units = []
            for n in range(16):
                for kh in range(2):
                    pr_ = slice(kh * 64, (kh + 1) * 64)
                    blocks = []
                    if n > 0:
                        blocks.append((kTp[kh][:, (n - 1) * 128:n * 128], Vaug[:, n - 1, kh, :], 0))
                    else:
                        blocks.append((kTp[kh][:, NLAT:NLAT + 128], Vaug[:, 16, kh, :], 2))
                    blocks.append((kTp[kh][:, n * 128:(n + 1) * 128], Vaug[:, n, kh, :], None))
                    if n < 15:
                        blocks.append((kTp[kh][:, (n + 1) * 128:(n + 2) * 128], Vaug[:, n + 1, kh, :], 1))
                    else:
                        blocks.append((kTp[kh][:, NLAT + 128:NLAT + 256], Vaug[:, 17, kh, :], 3))
                    blocks.append((kcTp[kh][:, 0:128], cVaug[:, 0, kh, :], None))
                    blocks.append((kcTp[kh][:, 128:256], cVaug[:, 1, kh, :], None))
                    units.append((qT[:, n, :, :].rearrange("p a b -> p (a b)"), blocks,
                                  mixT[pr_, 0:4, n * 128:(n + 1) * 128], kh))
            if first:
                for nq in range(2):
                    for kh in range(2):
                        pr_ = slice(kh * 64, (kh + 1) * 64)
                        blocks = [(kcTp[kh][:, 0:128], cVaug[:, 0, kh, :], None),
                                  (kcTp[kh][:, 128:256], cVaug[:, 1, kh, :], None)]
                        units.append((qcT[:, nq, :, :].rearrange("p a b -> p (a b)"), blocks,
                                      mixT[pr_, 0:4, NLAT + nq * 128:NLAT + (nq + 1) * 128], kh))
            steps = []
            for u, (qsrc, blocks, dest, kh) in enumerate(units):
                for i, (K, V, m) in enumerate(blocks):
                    steps.append((u, i, len(blocks), qsrc, K, V, m, dest, kh))
            LOOK = 3
            accs = {}
            for k in range(len(steps) + LOOK):
                if k < len(steps):
                    (u, i, nblk, qsrc, K, V, m, dest, kh) = steps[k]
                    sbk = k % 4
                    mm_group(ps[sbk][:, :], [(K, qsrc)], ["qT", ("kTp", kh), "qcT", ("kcTp", kh)], sbk)
                    pt = PT[k % NPT]
                    ACT(pt[:], ps[sbk][:, :], AF.Exp, [("ps", sbk)], [("PT", k % NPT)], scale=0.125)
                    if m is not None:
                        TT(pt[:], pt[:], masks[:, m, :], ALU.mult, [("PT", k % NPT), "masks"], [("PT", k % NPT)])
                if k >= LOOK:
                    kk = k - LOOK
                    (u, i, nblk, qsrc, K, V, m, dest, kh) = steps[kk]
                    if i == 0:
                        accs[u] = 4 + u % 3
                    acc = accs[u]
                    pt = PT[kk % NPT]
                    op("tensor", MM(ps[acc][:, :], V, pt[:], i == 0, i == nblk - 1),
                       reads=[("PT", kk % NPT), "Vaug", "cVaug"], writes=[("ps", acc)])
                    if i == nblk - 1:
                        TT(den[64:128, :], ps[acc][64:128, :], esink[64:128, kh, :], ALU.add,
                           [("ps", acc), "esink"], ["den"])
                        ACT(den[64:128, :], den[64:128, :], AF.Ln, ["den"], ["den"])
                        ACT(rec[0:64, :], den[64:128, :], AF.Exp, ["den"], ["rec"], scale=-1.0)
                        TT(dest, r4(ps[acc][0:64, :]), r4(rec[0:64, :]), ALU.mult, [("ps", acc), "rec"], ["mixT"])
            S.barrier(exclude=[("D", ("pc", L))])
            if stop_at(f"M2{L}"):
                stopped = True
                break

            DM("sync", G[:].rearrange("p a b -> p (a b)"), G_d[:, :], ("c", 13), writes=["G"])
            DM("sync", Wf_sb[:], w_f_d[L].rearrange("g c d -> c g d"), ("c", 14), writes=["Wf"])
            if first:
                DM("sync", CS256[:].rearrange("p a b c -> p (a b c)"), CS256_d[:, :], ("c", 15), writes=["CS256"])
            mxflat = Mx[:].rearrange("p a b c -> p (a b c)")
            op("vector", lambda e, mxflat=mxflat: e.memset(mxflat, 0.0), writes=["Mx"])
            urz = URf[64:128, :, :].rearrange("p a b -> p (a b)")
            op("scalar", lambda e, urz=urz: e.activation(out=urz, in_=urz, func=AF.Copy, scale=0.0), writes=["URz"])
            for g in range(8):
                for pq in range(2):
                    b = nb()
                    mm_group(ps[b][0:64, 0:64], [(CS64[:, pq, :], Wf_sb[:, g, :])], ["CS64", "Wf"], b)
                    o = (g % 2) * 64
                    CP("vector", Mx[o:o + 64, g // 2, pq, o:o + 64], ps[b][0:64, 0:64], [("ps", b)], ["Mx"])
            for j in range(4):
                for rk in range(2):
                    DM("sync", UR[rk * 32:(rk + 1) * 32, :, :],
                       gath[L][rk * 8192:(rk + 1) * 8192, :]
                       .rearrange("(rl c jj) e -> rl c (jj e)", c=64, jj=4)[:, :, j * 128:(j + 1) * 128],
                       ("UR", rk), reads=[("gath", L)], writes=[("UR", rk)])
                for q4 in range(32):
                    b = nb()
                    fns = [MM(ps[b][0:64, s_ * 128:(s_ + 1) * 128], URf[:, :, q4 * 4 + s_], F1[:, :], True, True)
                           for s_ in range(4)]
                    op("tensor", fns, reads=[("UR", 0), ("UR", 1), "F1", "URz"], writes=[("ps", b)])
                    src4 = ps[b][0:64, :].rearrange("p (a t k) -> p a t k", a=4, t=2)
                    ev = "scalar" if q4 % 2 == 0 else "vector"
                    CP(ev, Y2[0:64, q4 * 4:(q4 + 1) * 4, :], src4[:, :, 0, :], [("ps", b)], [("Y2", ev)])
                    CP(ev, Y2[64:128, q4 * 4:(q4 + 1) * 4, :], src4[:, :, 1, :], [("ps", b)], [("Y2", ev)])
                for k8 in range(8):
                    b = nb()
                    fns = [MM(ps[b][:, kk * 64:(kk + 1) * 64], Y2[:, :, k8 * 8 + kk], G[:, k8 * 8 + kk, :], True, True)
                           for kk in range(8)]
                    op("tensor", fns, reads=[("Y2", "scalar"), ("Y2", "vector"), "G"], writes=[("ps", b)])
                    for pq in range(2):
                        outv = PQ[:, pq, :].rearrange("p (k2 k1) -> p k2 k1", k1=64)[:, :, k8 * 8:(k8 + 1) * 8]
                        inv = ps[b][:, :].rearrange("p (kk pq k2) -> p pq k2 kk", kk=8, pq=2)[:, pq, :, :]
                        ev2 = "scalar" if k8 % 2 == 0 else "vector"
                        CP(ev2, outv, inv, [("ps", b)], [("PQ", ev2)])
                for t in range(4):
                    b = nb()
                    mm_group(ps[b][:, :], [(Mx[:, j, 0, :], PQ[:, 0, t * 512:(t + 1) * 512]),
                                           (Mx[:, j, 1, :], PQ[:, 1, t * 512:(t + 1) * 512])],
                             [("PQ", "scalar"), ("PQ", "vector"), "Mx"], b)
                    CP(evac_eng(), mixT[:, 4 + j, t * 512:(t + 1) * 512], ps[b][:, :], [("ps", b)], ["mixT"])
                if first:
                    for pq in range(2):
                        b = nb()
                        mm_group(ps[b][:, 0:256],
                                 [(Uc[:, kc2, j * 128:(j + 1) * 128], CS256[:, pq, kc2, :]) for kc2 in range(2)],
                                 ["Uc", "CS256"], b)
                        CP(evac_eng(), PQc[:, pq, :], ps[b][:, 0:256], [("ps", b)], ["PQc"])
                    b = nb()
                    mm_group(ps[b][:, 0:256], [(Mx[:, j, 0, :], PQc[:, 0, :]), (Mx[:, j, 1, :], PQc[:, 1, :])],
                             ["PQc", "Mx"], b)
                    CP(evac_eng(), mixT[:, 4 + j, NLAT:NTOK], ps[b][:, 0:256], [("ps", b)], ["mixT"])
            S.barrier(exclude=[("D", ("pc", L))])
            if stop_at(f"M3{L}"):
                stopped = True
                break

            for i in range(2):
                DM("gpsimd", w_out_sb[:, :, i * 512:(i + 1) * 512], wview(w_out_d, L)[:, :, i * 512:(i + 1) * 512],
                   ("wout", i), writes=[("w_out_sb", i)])
            tiles4 = [(t * 512, 512, 0) for t in range(4)] + ([(NLAT, NCTX, 1)] if first else [])
            m1_blocks = []
            if first:
                m2s = [A(89088 + 8192 * i_, [128, 8, 512], BF16) for i_ in range(3)]
                m1_blocks = list(range(12))

                def m1_load(bk):
                    DM("gpsimd", m2s[bk % 3], wview(w_ada1_d, 0)[:, :, bk * 512:(bk + 1) * 512], ("m2s", bk % 3),
                       writes=[("m2s", bk % 3)])

                for bk in range(3):
                    m1_load(bk)

            def m1_block():
                bk_ = m1_blocks.pop(0)
                b_ = nb()
                for s_ in range(4):
                    mm_group(ps[b_][:, s_ * 2:s_ * 2 + 2],
                             [(m2s[bk_ % 3][:, kc, s_ * 128:(s_ + 1) * 128], scb[:, kc, :]) for kc in range(8)],
                             [("m2s", bk_ % 3), "scb"], b_)
                CP("vector", m1stage[:, bk_ * 8:(bk_ + 1) * 8], ps[b_][:, 0:8], [("ps", b_)], ["m1stage"])
                if bk_ + 3 < 12:
                    m1_load(bk_ + 3)
            for ti, (c0, T, col) in enumerate(tiles4):
                mixf = mixfb[ti % 2]
                for oc in range(8):
                    b = nb()
                    mm_group(ps[b][:, 0:T],
                             [(w_out_sb[:, kc, oc * 128:(oc + 1) * 128], mixT[:, kc, c0:c0 + T]) for kc in range(8)],
                             ["mixT", ("w_out_sb", oc // 4)], b)
                    CP("scalar", mixf[:, oc, 0:T], ps[b][:, 0:T], [("ps", b)], [("mixf", ti % 2)])
                    bg(3)
                    if m1_blocks and (ti * 8 + oc) % 3 == 2:
                        m1_block()
                bg_drain()
                bgq.extend(postnorm_items(mixf[:, :, 0:T], ("mixf", ti % 2), c0, T, col, 1))
            bg_drain()
            while m1_blocks:
                m1_block()
            S.barrier()
            if stop_at(f"M4{L}"):
                stopped = True
                break

            if first:
                passes = [[(0, 384, 0, 0), (384, 384, 0, 384)], [(768, 384, 0, 0), (1152, 384, 0, 384)],
                          [(1536, 384, 0, 0), (1920, 128, 0, 384), (2048, 256, 1, 512)]]
            else:
                passes = [[(0, 384, 0, 0), (384, 384, 0, 384)], [(768, 384, 0, 0), (1152, 384, 0, 384)],
                          [(1536, 256, 0, 0), (1792, 256, 0, 256)]]
            def ffn_pre(p):
                items = []
                hfp = hfb[p % 2]
                for (c0_, T_, col_, loc_) in passes[p]:
                    items += prenorm_items(c0_, T_, col_, 2, 24,
                                           lambda kc, loc_=loc_, T_=T_, hfp=hfp: hfp[:, kc, loc_:loc_ + T_], ("hf", p % 2))
                return items

            bgq.extend(ffn_pre(0))
            bg_drain()
            for p, tiles in enumerate(passes):
                hf = hfb[p % 2]
                HF = ("hf", p % 2)
                for bk in range(6):
                    ncol = 512 if bk < 5 else 256
                    sa = nring()
                    DM("sync", w8(sa)[:, :, 0:ncol], wg_bf[L][bk][:, :, 0:ncol],
                       ("ring", sa), reads=[("wgbf", L, bk)], writes=[("ring", sa)])
                    su = nring()
                    DM("sync", w8(su)[:, :, 0:ncol], wu_bf[L][bk][:, :, 0:ncol],
                       ("ring", su), reads=[("wubf", L, bk)], writes=[("ring", su)])
                    for s_ in range(ncol // 128):
                        jj = bk * 4 + s_
                        for g0 in range(0, len(tiles), 2):
                            grp = tiles[g0:g0 + 2]
                            for tix, (c0, T, col, loc) in enumerate(grp):
                                bg_ = nb()
                                mm_group(ps[bg_][:, 0:T],
                                         [(w8(sa)[:, kc, s_ * 128:(s_ + 1) * 128], hf[:, kc, loc:loc + T])
                                          for kc in range(8)], [HF, ("ring", sa)], bg_)
                                ACT(tmp[tix][:, 0:T], ps[bg_][:, 0:T], AF.Silu, [("ps", bg_)], [("tmp", tix)])
                            for tix, (c0, T, col, loc) in enumerate(grp):
                                bu = nb()
                                mm_group(ps[bu][:, 0:T],
                                         [(w8(su)[:, kc, s_ * 128:(s_ + 1) * 128], hf[:, kc, loc:loc + T])
                                          for kc in range(8)], [HF, ("ring", su)], bu)
                                TT(act[:, jj, loc:loc + T], tmp[tix][:, 0:T], ps[bu][:, 0:T], ALU.mult,
                                   [("tmp", tix), ("ps", bu)], ["act"])
                            bg(len(grp))
                bg_drain()
                if p + 1 < len(passes):
                    bgq.extend(ffn_pre(p + 1))
                for bk in range(4):
                    sd = nring()
                    DM("sync", wd(sd), wd_bf[L][bk], ("ring", sd), reads=[("wdbf", L, bk)], writes=[("ring", sd)])
                    for s_ in range(2):
                        oc = bk * 2 + s_
                        for (c0, T, col, loc) in tiles:
                            b = nb()
                            mm_group(ps[b][:, 0:T],
                                     [(wd(sd)[:, j2, s_ * 128:(s_ + 1) * 128], act[:, j2, loc:loc + T]) for j2 in range(22)],
                                     ["act", ("ring", sd)], b)
                            CP("scalar", fout[:, oc, loc:loc + T], ps[b][:, 0:T], [("ps", b)], ["fout"])
                            bg(3)
                bg_drain()
                for (c0, T, col, loc) in tiles:
                    bgq.extend(postnorm_items(fout[:, :, loc:loc + T], "fout", c0, T, col, 3))
            bg_drain()
            S.barrier()

        for t in range(4):
            DM("sync", outT_d[:, :, t * 512:(t + 1) * 512].rearrange("k p n -> p k n"), xT[:, :, t * 512:(t + 1) * 512],
               "out", reads=xres(t * 512, 512), writes=[("out", t)])
        S._waits("sync", [S.latest[("D", "out")]])

        sems = {}
        for i, k in enumerate(S.all_semkeys()):
            sems[k] = es.enter_context(nc.semaphore(f"s{i}"))
        with nc.Block() as block:
            @block.tensor
            def _(e):
                S.emit("tensor", e, sems)

            @block.vector
            def _(e):
                S.emit("vector", e, sems)

            @block.scalar
            def _(e):
                S.emit("scalar", e, sems)

            @block.gpsimd
            def _(e):
                S.emit("gpsimd", e, sems)

            @block.sync
            def _(e):
                S.emit("sync", e, sems)
    return nc


def _tables(half):
    t = np.arange(NLAT) + half * NLAT
    row = (t // 64).astype(np.float32)
    col = (t % 64).astype(np.float32)
    freqs = (np.float32(10000.0) ** (-np.arange(16, dtype=np.float32) / np.float32(16))).astype(np.float32)
    ang = np.concatenate([row[:, None] * freqs, col[:, None] * freqs], axis=-1).astype(np.float32)
    cos, sin = np.cos(ang), np.sin(ang)
    d = np.arange(128) % 64
    pair = d // 2
    ropeC = cos[:, pair].T
    sgn = np.where(d % 2 == 0, -1.0, 1.0)[:, None]
    ropeS = sin[:, pair].T * sgn
    rope = np.concatenate([ropeC, ropeS], axis=1).astype(NPBF)
    kj = np.arange(128)[:, None]
    qi = np.arange(128)[None, :]
    ML = (qi <= kj).astype(np.float32)
    MR = (kj <= qi).astype(np.float32)
    ms = [ML, MR, ML * (1.0 if half == 1 else 0.0), MR * (1.0 if half == 0 else 0.0)]
    masks = np.concatenate([np.tile(m, (1, 4)) for m in ms], axis=1).astype(NPBF)
    c = np.arange(64, dtype=np.float64)[:, None, None]
    k1 = np.arange(64, dtype=np.float64)[None, :, None]
    k2 = (half * 32 + np.arange(32, dtype=np.float64))[None, None, :]
    th = 2 * np.pi * c * (k1 + 64 * k2) / 4096.0
    Gr = np.concatenate([np.cos(th), np.sin(th)], axis=2) / 512.0
    Gi = np.concatenate([np.sin(th), -np.cos(th)], axis=2) / 512.0
    G = np.concatenate([Gr.reshape(64, -1), Gi.reshape(64, -1)], axis=0).astype(NPBF)
    return rope, masks, G


def kernel(x, c, ctx, c_ctx, w_ada, b_ada, norm_pre_mix, norm_post_mix, norm_pre_ffn, norm_post_ffn,
           w_in, w_out, w_fourier, sink, w_gate, w_up, w_down):
    f = lambda a: np.ascontiguousarray(np.asarray(a, dtype=np.float32))
    x, c, ctx, c_ctx = f(x), f(c), f(ctx), f(c_ctx)
    w_ada, b_ada, w_in, w_out = f(w_ada), f(b_ada), f(w_in), f(w_out)
    w_fourier, sink, w_gate, w_up, w_down = f(w_fourier), f(sink), f(w_gate), f(w_up), f(w_down)
    gs = [f(norm_pre_mix), f(norm_post_mix), f(norm_pre_ffn), f(norm_post_ffn)]
    qperm = np.concatenate([np.r_[cc * 64:(cc + 1) * 64, (4 + cc) * 64:(5 + cc) * 64] for cc in range(4)])
    w_in_p = np.ascontiguousarray(np.concatenate([w_in[:, :, qperm], w_in[:, :, 512:]], axis=2))
    w_out_p = np.ascontiguousarray(np.concatenate([w_out[:, qperm, :], w_out[:, 512:, :]], axis=1))
    b_adaT = np.ascontiguousarray(b_ada.reshape(2, 48, 128).transpose(2, 0, 1).reshape(128, 96))
    gains = np.stack(gs, axis=1)
    gains = np.ascontiguousarray(gains.reshape(2, 4, 8, 128).transpose(3, 0, 1, 2).reshape(128, 64))
    sinkb = np.ascontiguousarray(
        np.broadcast_to(sink.reshape(1, 2, 2, 4, 1), (128, 2, 2, 4, 128)).reshape(128, 2048))
    permT = np.zeros((128, 128), np.float32)
    permT[np.arange(128), np.arange(128) ^ 1] = 1.0
    permT = permT.astype(NPBF)
    r = np.arange(64, dtype=np.float64)
    ph = 2 * np.pi * np.outer(r, r) / 64.0
    F1 = np.concatenate([np.cos(ph), -np.sin(ph)], axis=1)
    F1 = np.concatenate([F1, F1], axis=0).astype(NPBF)
    CS64 = np.concatenate([np.cos(ph), -np.sin(ph)], axis=1).astype(np.float32)
    n256 = np.arange(256, dtype=np.float64)
    ph2 = 2 * np.pi * np.outer(n256, n256) / 256.0
    cs = np.stack([np.cos(ph2), np.sin(ph2)], axis=0) / 128.0
    CS256 = np.ascontiguousarray(cs.reshape(2, 2, 128, 256).transpose(2, 0, 1, 3).reshape(128, -1)).astype(NPBF)
    tabs = [_tables(0), _tables(1)]
    w_ada_halves = [np.ascontiguousarray(w_ada[0:1, :, h * 3072:(h + 1) * 3072]) for h in range(2)]
    w_ada1 = np.ascontiguousarray(w_ada[1:2])

    in_maps = []
    for core in range(8):
        b, half = core // 2, core % 2
        xs = x[b, half * NLAT:(half + 1) * NLAT, :]
        xT = np.ascontiguousarray(xs.T.reshape(8, 128, NLAT))
        xcT = np.ascontiguousarray(ctx[b].T.reshape(8, 128, NCTX))
        cv = np.stack([c[b], c_ctx], axis=1)
        cvec = np.ascontiguousarray(cv.reshape(8, 128, 2).transpose(1, 0, 2).reshape(128, 16))
        w_adas = w_ada_halves[half]
        rope, masks, G = tabs[half]
        in_maps.append({
            "xT": xT, "xcT": xcT, "cvec": cvec, "w_adas": w_adas, "w_ada1": w_ada1, "b_adaT": b_adaT,
            "gains": gains,
            "w_in": w_in_p, "w_out": w_out_p, "w_four": w_fourier, "sinkb": sinkb,
            "w_gate": w_gate, "w_up": w_up, "w_down": w_down,
            "permT": permT, "rope": rope, "masks": masks, "F1": F1, "G": G, "CS64": CS64, "CS256": CS256,
        })
    nc = build_program()
    res = run_bass_kernel_spmd(nc, in_maps, core_ids=list(range(8)))
    out = np.empty((4, 4096, 1024), np.float32)
    for core in range(8):
        b, half = core // 2, core % 2
        oT = np.asarray(res.results[core]["outT"]).reshape(1024, NLAT)
        out[b, half * NLAT:(half + 1) * NLAT, :] = oT.T
    return out
```
